# Optimizing a Trainium2 kernel written in Bass

```python
import math
import jax
import jax.numpy as jnp
from jax import lax
import numpy as np

D_MODEL = 1024
BATCH = 32
SEQ = 256
DEPTH = 2
DEC_BATCH = 2
DEC_SEQ = 4096
PAST_LEN = 512

GRID_W = 64
HEAD_DIM = 64
D_FF = 2816
N_MOD = 9
W_POOL = D_MODEL // 2
W_CONV = D_MODEL - W_POOL
POOL_WINDOWS = (2, 4, 8, 16)
N_POOL_GROUPS = len(POOL_WINDOWS)
POOL_G = W_POOL // N_POOL_GROUPS
CONV_K = 3
W_NAT = D_MODEL // 2
W_DIFF = D_MODEL - W_NAT
H_NAT = W_NAT // HEAD_DIM
H_DIFF = W_DIFF // (2 * HEAD_DIM)
NAT_WIN_R = 8
NAT_WIN_C = 16
ROPE_THETA = 10000.0
Q_BLOCK = 128
N_EVEN = (DEPTH + 1) // 2
N_ODD = DEPTH // 2
EVEN_IN = W_POOL + 3 * W_CONV
ODD_IN = 3 * W_NAT + 3 * W_DIFF
ATTN_SCALE = HEAD_DIM ** -0.5
EPS = 1e-6

kernel_name = 'hybrid_pool_conv_nat_diff_prefix_dit_step'


def rmsnorm(x, g):
    xf = x.astype(jnp.float32)
    y = xf * lax.rsqrt(jnp.mean(xf * xf, axis=-1, keepdims=True) + EPS)
    return (y * g.astype(jnp.float32)).astype(x.dtype)


def adaln_mods(cond, mod_w, mod_b):
    m = jax.nn.silu(cond) @ mod_w + mod_b
    return jnp.split(m[:, None, :], N_MOD, axis=-1)


def modulated_norm(x, shift, scale, g):
    return rmsnorm(x, g) * (1.0 + scale) + shift


def swiglu(h, w_in, w_out):
    a, b = jnp.split(h @ w_in, 2, axis=-1)
    return (jax.nn.silu(a) * b) @ w_out


def ffn_half_step(x, mods, g, w_in, w_out):
    shift, scale, gate = mods
    return x + 0.5 * gate * swiglu(modulated_norm(x, shift, scale, g), w_in, w_out)


def pool_mixer(xa, pool_w, pool_scale):
    b, n, _ = xa.shape
    xg = xa.reshape(b, n, N_POOL_GROUPS, POOL_G)
    t = np.arange(n)[None, :]
    half = np.array(POOL_WINDOWS)[:, None] // 2
    lo = np.clip(t - half, 0, n - 1).T
    hi = np.clip(t + half - 1, 0, n - 1).T
    cnt = (hi - lo + 1).astype(np.float32)
    gidx = np.arange(N_POOL_GROUPS)[None, :]
    cs = jnp.pad(jnp.cumsum(xg.astype(jnp.float32), axis=1), ((0, 0), (1, 0), (0, 0), (0, 0)))
    win_sum = cs[:, hi + 1, gidx] - cs[:, lo, gidx]
    mean = win_sum / jnp.asarray(cnt)[None, :, :, None]
    d = (mean - xg.astype(jnp.float32)).astype(xa.dtype)
    y = jnp.einsum('bngc,gcd->bngd', d, pool_w)
    return y.reshape(b, n, W_POOL) * pool_scale


def short_conv_mixer(u, conv_w):
    h, gb, gc = jnp.split(u, 3, axis=-1)
    z = gc * h
    n = z.shape[1]
    zp = jnp.pad(z, ((0, 0), (CONV_K // 2, CONV_K // 2), (0, 0)))
    y = sum(zp[:, j:j + n] * conv_w[j] for j in range(CONV_K))
    return gb * y


def even_mixer(h, w_in, pool_w, pool_scale, conv_w, w_out):
    u = h @ w_in
    ya = pool_mixer(u[..., :W_POOL], pool_w, pool_scale)
    yb = short_conv_mixer(u[..., W_POOL:], conv_w)
    return jnp.concatenate([ya, yb], axis=-1) @ w_out


def axial_rope_tables(n):
    t = np.arange(n)
    row = (t // GRID_W).astype(np.float64)
    col = (t % GRID_W).astype(np.float64)
    quarter = HEAD_DIM // 4
    inv = 1.0 / (ROPE_THETA ** (np.arange(quarter) / quarter))
    ang = np.stack([row[:, None] * inv[None], col[:, None] * inv[None]], axis=1)
    return jnp.asarray(np.cos(ang).astype(np.float32)), jnp.asarray(np.sin(ang).astype(np.float32))


def apply_axial_rope(x, cos, sin):
    shp = x.shape
    xr = x.reshape(shp[:-1] + (2, 2, HEAD_DIM // 4)).astype(jnp.float32)
    c = cos[None, :, None, None]
    s = sin[None, :, None, None]
    x1, x2 = xr[..., 0, :], xr[..., 1, :]
    out = jnp.stack([x1 * c - x2 * s, x2 * c + x1 * s], axis=-2)
    return out.reshape(shp).astype(x.dtype)


def _split_blocks(q):
    b, n = q.shape[:2]
    return jnp.moveaxis(q.reshape((b, n // Q_BLOCK, Q_BLOCK) + q.shape[2:]), 1, 0)


def _merge_blocks(o):
    nb, b = o.shape[:2]
    return jnp.moveaxis(o, 0, 1).reshape((b, nb * Q_BLOCK) + o.shape[3:])


def softmax_attn(q, k, v):
    def one(qb):
        s = jnp.einsum('bqhd,bkhd->bhqk', qb, k).astype(jnp.float32)
        p = jax.nn.softmax(s, axis=-1).astype(v.dtype)
        return jnp.einsum('bhqk,bkhd->bqhd', p, v)
    return _merge_blocks(lax.map(one, _split_blocks(q)))


def diff_attn(q, k, v, lam):
    def one(qb):
        s = jnp.einsum('bqhjd,bkhjd->bhjqk', qb, k).astype(jnp.float32)
        p = jax.nn.softmax(s, axis=-1)
        a = (p[:, :, 0] - lam * p[:, :, 1]).astype(v.dtype)
        return jnp.einsum('bhqk,bkhe->bqhe', a, v)
    return _merge_blocks(lax.map(one, _split_blocks(q)))


def diff_lambda_value(lam_p, lam_init):
    lp = lam_p.astype(jnp.float32)
    return jnp.exp(jnp.sum(lp[0] * lp[1])) - jnp.exp(jnp.sum(lp[2] * lp[3])) + lam_init


def diff_heads_out(o, lam_init, g):
    b, n = o.shape[:2]
    return (rmsnorm(o, g) * (1.0 - lam_init)).reshape(b, n, W_DIFF)


def nat_latent_attn(q, k, v, kc, vc, rpb):
    b, n, h, d = q.shape
    rows = n // GRID_W
    wr = min(NAT_WIN_R, rows)
    r = np.arange(rows)
    r0 = np.clip(r - wr // 2, 0, rows - wr)
    key_rows = r0[:, None] + np.arange(wr)[None]
    col = np.arange(GRID_W)
    c0 = np.clip(col - NAT_WIN_C // 2, 0, GRID_W - NAT_WIN_C)
    key_cols = c0[:, None] + np.arange(NAT_WIN_C)[None]
    dr = key_rows - r[:, None] + (NAT_WIN_R - 1)
    dc = key_cols - col[:, None] + (NAT_WIN_C - 1)
    col_bias = rpb[:, :, dc]
    kg = k.reshape(b, rows, GRID_W, h, d)
    vg = v.reshape(b, rows, GRID_W, h, d)
    qg = jnp.moveaxis(q.reshape(b, rows, GRID_W, h, d), 1, 0)
    n_loc = wr * NAT_WIN_C

    def one(args):
        q_row, kr, drr = args
        k_win = jnp.take(jnp.take(kg, kr, axis=1), key_cols, axis=2)
        v_win = jnp.take(jnp.take(vg, kr, axis=1), key_cols, axis=2)
        bias = jnp.moveaxis(jnp.take(col_bias, drr, axis=1), 1, 2)
        s_loc = jnp.einsum('bqhd,brqkhd->bhqrk', q_row, k_win).astype(jnp.float32) + bias[None].astype(jnp.float32)
        s_ctx = jnp.einsum('bqhd,bmhd->bhqm', q_row, kc).astype(jnp.float32)
        p = jax.nn.softmax(jnp.concatenate([s_loc.reshape(b, h, GRID_W, n_loc), s_ctx], axis=-1), axis=-1)
        p_loc = p[..., :n_loc].reshape(b, h, GRID_W, wr, NAT_WIN_C).astype(v.dtype)
        p_ctx = p[..., n_loc:].astype(v.dtype)
        return (jnp.einsum('bhqrk,brqkhd->bqhd', p_loc, v_win)
                + jnp.einsum('bhqm,bmhd->bqhd', p_ctx, vc))

    out = lax.map(one, (qg, jnp.asarray(key_rows, jnp.int32), jnp.asarray(dr, jnp.int32)))
    return jnp.moveaxis(out, 0, 1).reshape(b, n, h, d)


def odd_project(h, w_in):
    b, n, _ = h.shape
    cuts = np.cumsum([W_NAT, W_NAT, W_NAT, W_DIFF, W_DIFF]).tolist()
    nq, nk, nv, dq, dk, dv = jnp.split(h @ w_in, cuts, axis=-1)
    return (nq.reshape(b, n, H_NAT, HEAD_DIM), nk.reshape(b, n, H_NAT, HEAD_DIM),
            nv.reshape(b, n, H_NAT, HEAD_DIM), dq.reshape(b, n, H_DIFF, 2, HEAD_DIM),
            dk.reshape(b, n, H_DIFF, 2, HEAD_DIM), dv.reshape(b, n, H_DIFF, 2 * HEAD_DIM))


def odd_mixer_context(h, w_in, lam_p, dnorm, w_out, layer_idx):
    b, n, _ = h.shape
    nq, nk, nv, dq, dk, dv = odd_project(h, w_in)
    lam_init = 0.8 - 0.6 * math.exp(-0.3 * layer_idx)
    lam = diff_lambda_value(lam_p, lam_init)
    o_nat = softmax_attn(nq * ATTN_SCALE, nk, nv).reshape(b, n, W_NAT)
    o_diff = diff_heads_out(diff_attn(dq * ATTN_SCALE, dk, dv, lam), lam_init, dnorm)
    y = jnp.concatenate([o_nat, o_diff], axis=-1) @ w_out
    return y, nk, nv, dk, dv


def odd_mixer_latent(h, kc, vc, dkc, dvc, w_in, rpb, lam_p, dnorm, w_out, layer_idx, cos, sin):
    b, n, _ = h.shape
    nq, nk, nv, dq, dk, dv = odd_project(h, w_in)
    dq = apply_axial_rope(dq, cos, sin)
    dk = apply_axial_rope(dk, cos, sin)
    lam_init = 0.8 - 0.6 * math.exp(-0.3 * layer_idx)
    lam = diff_lambda_value(lam_p, lam_init)
    o_nat = nat_latent_attn(nq * ATTN_SCALE, nk, nv, kc, vc, rpb).reshape(b, n, W_NAT)
    k_all = jnp.concatenate([dk, dkc], axis=1)
    v_all = jnp.concatenate([dv, dvc], axis=1)
    o_diff = diff_heads_out(diff_attn(dq * ATTN_SCALE, k_all, v_all, lam), lam_init, dnorm)
    return jnp.concatenate([o_nat, o_diff], axis=-1) @ w_out


def setup_inputs(seed: int = 0) -> dict:
    key = jax.random.key(seed)
    ks = jax.random.split(key, 32)
    f32 = jnp.float32
    D = D_MODEL

    def nrm(k, shape, scale):
        return jax.random.normal(k, shape, f32) * scale

    def gain(k, shape):
        return 1.0 + 0.02 * jax.random.normal(k, shape, f32)

    return {
        'x_prompt': nrm(ks[0], (BATCH, SEQ, D), 1.0),
        'x_sample': nrm(ks[1], (DEC_BATCH, DEC_SEQ, D), 1.0),
        'cache_nat_k': nrm(ks[2], (DEC_BATCH, N_ODD, PAST_LEN, H_NAT, HEAD_DIM), 1.0),
        'cache_nat_v': nrm(ks[3], (DEC_BATCH, N_ODD, PAST_LEN, H_NAT, HEAD_DIM), 1.0),
        'cache_diff_k': nrm(ks[4], (DEC_BATCH, N_ODD, PAST_LEN, H_DIFF, 2, HEAD_DIM), 1.0),
        'cache_diff_v': nrm(ks[5], (DEC_BATCH, N_ODD, PAST_LEN, H_DIFF, 2 * HEAD_DIM), 1.0),
        'c': nrm(ks[6], (DEC_BATCH, D), 1.0),
        'c_ctx': nrm(ks[7], (D,), 1.0),
        'mod_w': nrm(ks[8], (DEPTH, D, N_MOD * D), 0.5 * D ** -0.5),
        'mod_b': nrm(ks[9], (DEPTH, N_MOD * D), 0.01),
        'norm_ffn1': gain(ks[10], (DEPTH, D)),
        'ffn1_w_in': nrm(ks[11], (DEPTH, D, 2 * D_FF), D ** -0.5),
        'ffn1_w_out': nrm(ks[12], (DEPTH, D_FF, D), D_FF ** -0.5),
        'norm_mix': gain(ks[13], (DEPTH, D)),
        'even_w_in': nrm(ks[14], (N_EVEN, D, EVEN_IN), D ** -0.5),
        'pool_w': nrm(ks[15], (N_EVEN, N_POOL_GROUPS, POOL_G, POOL_G), POOL_G ** -0.5),
        'pool_scale': gain(ks[16], (N_EVEN, W_POOL)),
        'conv_w': nrm(ks[17], (N_EVEN, CONV_K, W_CONV), CONV_K ** -0.5),
        'odd_w_in': nrm(ks[18], (N_ODD, D, ODD_IN), D ** -0.5),
        'nat_rpb': nrm(ks[19], (N_ODD, H_NAT, 2 * NAT_WIN_R - 1, 2 * NAT_WIN_C - 1), 0.1),
        'diff_lambda': nrm(ks[20], (N_ODD, 4, HEAD_DIM), 0.1),
        'diff_norm': gain(ks[21], (N_ODD, 2 * HEAD_DIM)),
        'mix_w_out': nrm(ks[22], (DEPTH, D, D), D ** -0.5),
        'norm_ffn2': gain(ks[23], (DEPTH, D)),
        'ffn2_w_in': nrm(ks[24], (DEPTH, D, 2 * D_FF), D ** -0.5),
        'ffn2_w_out': nrm(ks[25], (DEPTH, D_FF, D), D_FF ** -0.5),
        'final_norm': gain(ks[26], (D,)),
    }


def reference(x_prompt, x_sample, cache_nat_k, cache_nat_v, cache_diff_k, cache_diff_v, c, c_ctx,
              mod_w, mod_b, norm_ffn1, ffn1_w_in, ffn1_w_out, norm_mix, even_w_in, pool_w,
              pool_scale, conv_w, odd_w_in, nat_rpb, diff_lambda, diff_norm, mix_w_out,
              norm_ffn2, ffn2_w_in, ffn2_w_out, final_norm):
    cos, sin = axial_rope_tables(x_sample.shape[1])
    ctx, lat = x_prompt, x_sample
    new_nk, new_nv, new_dk, new_dv = [], [], [], []
    for l in range(DEPTH):
        m_ctx = adaln_mods(c_ctx[None, :], mod_w[l], mod_b[l])
        m_lat = adaln_mods(c, mod_w[l], mod_b[l])
        ctx = ffn_half_step(ctx, m_ctx[0:3], norm_ffn1[l], ffn1_w_in[l], ffn1_w_out[l])
        lat = ffn_half_step(lat, m_lat[0:3], norm_ffn1[l], ffn1_w_in[l], ffn1_w_out[l])
        h_ctx = modulated_norm(ctx, m_ctx[3], m_ctx[4], norm_mix[l])
        h_lat = modulated_norm(lat, m_lat[3], m_lat[4], norm_mix[l])
        if l % 2 == 0:
            e = l // 2
            y_ctx = even_mixer(h_ctx, even_w_in[e], pool_w[e], pool_scale[e], conv_w[e], mix_w_out[l])
            y_lat = even_mixer(h_lat, even_w_in[e], pool_w[e], pool_scale[e], conv_w[e], mix_w_out[l])
        else:
            o = l // 2
            y_ctx, nk, nv, dk, dv = odd_mixer_context(h_ctx, odd_w_in[o], diff_lambda[o], diff_norm[o],
                                                      mix_w_out[l], l)
            new_nk.append(nk)
            new_nv.append(nv)
            new_dk.append(dk)
            new_dv.append(dv)
            y_lat = odd_mixer_latent(h_lat, cache_nat_k[:, o], cache_nat_v[:, o], cache_diff_k[:, o],
                                     cache_diff_v[:, o], odd_w_in[o], nat_rpb[o], diff_lambda[o],
                                     diff_norm[o], mix_w_out[l], l, cos, sin)
        ctx = ctx + m_ctx[5] * y_ctx
        lat = lat + m_lat[5] * y_lat
        ctx = ffn_half_step(ctx, m_ctx[6:9], norm_ffn2[l], ffn2_w_in[l], ffn2_w_out[l])
        lat = ffn_half_step(lat, m_lat[6:9], norm_ffn2[l], ffn2_w_in[l], ffn2_w_out[l])
    y_prompt = rmsnorm(ctx, final_norm)
    y_sample = rmsnorm(lat, final_norm)
    new_nat_k = jnp.stack(new_nk, axis=1)
    new_nat_v = jnp.stack(new_nv, axis=1)
    new_diff_k = jnp.stack(new_dk, axis=1)
    new_diff_v = jnp.stack(new_dv, axis=1)
    return (y_prompt, y_sample, new_nat_k, new_nat_v, new_diff_k, new_diff_v)
```

```python
import contextlib
import math
import numpy as np
import ml_dtypes
import concourse.bass as bass
import concourse.mybir as mybir
from concourse.bass_utils import run_bass_kernel_spmd

F32 = mybir.dt.float32
BF16 = mybir.dt.bfloat16
AF = mybir.ActivationFunctionType
ALU = mybir.AluOpType

D = 1024
DFF = 2816
NJ = 22
EPS = 1e-6
NTOK = 2064
GROUPS = [list(range(0, 6)), list(range(6, 12)), list(range(12, 17)), list(range(17, 22))]
TILES_ALL = [(0, 512, 0), (512, 512, 0), (1024, 512, 1), (1536, 512, 1)]
TILE_HALO = (2048, 16, 1)
NEG = -30000.0
CUT = 99


class _Cut(Exception):
    pass

PV_COND = 0
PV_MODB = 24
PV_GAIN = 60
PV_PSC = 108
PV_CONV = 112
PV_HMASK = 124
PV_SELP = 140
PV_SELN = 144
PV_SELB = 148
PV_N = 152
BV_FN = 0
BV_DN = 1024
BV_LAM = 1152
BV_CM = 1408
BV_N = 1472


class Sched:
    def __init__(self, nc, es):
        self.nc = nc
        self.eng = dict(pe=nc.tensor, act=nc.scalar, dve=nc.vector, pool=nc.gpsimd, sp=nc.sync)
        self.sem = {k: es.enter_context(nc.semaphore("s_" + k)) for k in ("pe", "act", "dve", "pool")}
        self.seq = dict(pe=0, act=0, dve=0, pool=0)
        self.known = {k: {} for k in self.eng}
        self.st = {}
        self.dsem = {}
        self.dval = {}
        self.dnext = {}
        for q, n in (("sp", 20), ("pool", 20)):
            self.dsem[q] = [es.enter_context(nc.semaphore(f"d_{q}{i}")) for i in range(n)]
            self.dval[q] = [0] * n
            self.dnext[q] = 0
        self.ccsem = es.enter_context(nc.semaphore("ccsem"))
        self.ccsem2 = es.enter_context(nc.semaphore("ccsem2"))
        self.out_toks = []

    def wait(self, eng, tok):
        sem, val, src = tok
        if src == "pe" and eng == "pe":
            return
        kn = self.known[eng]
        key = id(sem)
        if kn.get(key, 0) >= val:
            return
        self.eng[eng].wait_ge(sem, val)
        kn[key] = val

    def _deps(self, r, w):
        deps = []
        for k in r:
            s = self.st.get(k)
            if s:
                deps += list(s[0].values())
        for k in w:
            s = self.st.get(k)
            if s:
                deps += list(s[0].values())
                deps += list(s[1].values())
        return deps

    def _record(self, tok, r, w):
        sem, val, src = tok
        kk = src if src else (id(sem), val)
        for k in r:
            self.st.setdefault(k, [{}, {}])[1][kk] = tok
        for k in w:
            self.st[k] = [{kk: tok}, {}]

    def op(self, eng, fn, r=(), w=(), inc=True):
        for t in self._deps(r, w):
            self.wait(eng, t)
        ins = fn(self.eng[eng])
        if inc:
            self.seq[eng] += 1
            ins.then_inc(self.sem[eng], 1)
            tok = (self.sem[eng], self.seq[eng], eng)
        else:
            tok = (self.sem[eng], self.seq[eng] + 1, eng)
        self._record(tok, r, w)
        return tok

    def dma(self, q, out, in_, r=(), w=(), **kw):
        for t in self._deps(r, w):
            self.wait(q, t)
        i = self.dnext[q]
        self.dnext[q] = (i + 1) % len(self.dsem[q])
        sem = self.dsem[q][i]
        if self.dval[q][i] > 0:
            self.wait(q, (sem, self.dval[q][i], None))
        ins = self.eng[q].dma_start(out=out, in_=in_, **kw)
        self.dval[q][i] += 16
        ins.then_inc(sem, 16)
        tok = (sem, self.dval[q][i], None)
        self._record(tok, r, w)
        return tok

    def all_toks(self):
        toks = [(self.sem[e], self.seq[e], e) for e in self.seq if self.seq[e] > 0]
        for q in self.dsem:
            for i, s in enumerate(self.dsem[q]):
                if self.dval[q][i] > 0:
                    toks.append((s, self.dval[q][i], None))
        return toks

    def barrier(self):
        toks = self.all_toks()
        for e in self.eng:
            for t in toks:
                self.wait(e, t)
        self.st = {}


def build_program(stage="full", debug=False):
    nc = bass.Bass("TRN2", target_bir_lowering=False)
    es = contextlib.ExitStack()

    def din(name, shape, dt=F32):
        return nc.dram_tensor(name, list(shape), dt, kind="ExternalInput").ap()

    def dout(name, shape, dt=F32):
        return nc.dram_tensor(name, list(shape), dt, kind="ExternalOutput").ap()

    xin = din("xin", [NTOK, D])
    pvec_d = din("pvec", [128, PV_N])
    bvec_d = din("bvec", [128, BV_N])
    modw_d = din("modw", [4, 128, 8 * 1152])
    w1in_d = din("w1in", [2, 2, NJ, 128, 2048])
    w1out_d = din("w1out", [2, 2, NJ, 128, 1024])
    ewin_d = din("ewin", [8, 128, 2048])
    poolw_d = din("poolw", [4, 128, 128])
    mixw_d = din("mixw", [2, 8, 128, 1024])
    owin_d = din("owin", [8, 128, 4096])
    ident_d = din("ident", [128, 128])
    icc_d = din("icc", [128, 4 * 256])
    icl_d = din("icl", [128, 4 * 1024])
    ropec_d = din("ropec", [128, 1024])
    ropes_d = din("ropes", [128, 1024])
    tbsrc_d = din("tbsrc", [64, 8 * 15 * 64])
    amat_d = din("amat", [24, 1536])
    bmat_d = din("bmat", [24, 1024])
    cnkT_d = din("cnkT", [512, 512])
    cnv_d = din("cnv", [512, 512])
    cdkT_d = din("cdkT", [512, 512])
    cdv_d = din("cdv", [512, 512])

    y_d = dout("y", [2048, D])
    onk_d = dout("onk", [1024, 512])
    onv_d = dout("onv", [1024, 512])
    odk_d = dout("odk", [1024, 512])
    odv_d = dout("odv", [1024, 512])
    dbg_d = dout("dbg", [8, 128, 2048]) if debug else None

    mg_in = nc.dram_tensor("mg_in", [128, 108], F32, kind="Internal").ap()
    mg_out = nc.dram_tensor("mg_out", [512, 108], F32, kind="Internal", addr_space="Local").ap()
    gin_f = [nc.dram_tensor(f"gin{i}", [512, 512], F32, kind="Internal").ap() for i in range(4)]
    gout_f = [nc.dram_tensor(f"gout{i}", [2048, 512], F32, kind="Internal", addr_space="Local").ap() for i in range(4)]
    gin_b = [g_.bitcast(BF16) for g_ in gin_f]
    gout_b = [g_.bitcast(BF16) for g_ in gout_f]

    S = Sched(nc, es)
    cut_hit = []

    _uid = [0]

    def sb(stack, name, shape, dt=F32):
        _uid[0] += 1
        return stack.enter_context(nc.sbuf_tensor(f"{name}_{_uid[0]}", list(shape), dt))

    X = sb(es, "X", [128, 8, NTOK])
    pvec = sb(es, "pvec_s", [128, PV_N])
    mods = sb(es, "mods", [128, 2, 72, 2])
    gsv = sb(es, "gsv", [128, 2, 3, 8, 2])
    ghv = sb(es, "ghv", [128, 2, 2, 8, 2])
    ident = sb(es, "ident_s", [128, 128])
    identb = sb(es, "identb", [128, 128], BF16)
    onesb = sb(es, "onesb", [128, 128], BF16)
    PS = [es.enter_context(nc.psum_tensor(f"ps{i}", [128, 512], F32)) for i in range(8)]

    def pk(i):
        return ("ps", i)

    def mm(out, lhsT, rhs, start, stop, r, w, inc):
        return S.op("pe", lambda e: e.matmul(out, lhsT=lhsT, rhs=rhs, start=start, stop=stop), r=r, w=w, inc=inc)

    def act(out, in_, func, r, w, bias=None, scale=None, accum_out=None):
        kw = {}
        if bias is not None:
            kw["bias"] = bias
        if scale is not None:
            kw["scale"] = scale
        if accum_out is not None:
            kw["accum_out"] = accum_out
        return S.op("act", lambda e: e.activation(out=out, in_=in_, func=func, **kw), r=r, w=w)

    def tt(out, in0, in1, op, r, w, eng="dve"):
        return S.op(eng, lambda e: e.tensor_tensor(out=out, in0=in0, in1=in1, op=op), r=r, w=w)

    def ts(out, in0, s1, op0, r, w, s2=None, op1=None, eng="dve"):
        if op1 is None:
            return S.op(eng, lambda e: e.tensor_scalar(out=out, in0=in0, scalar1=s1, scalar2=None, op0=op0), r=r, w=w)
        return S.op(eng, lambda e: e.tensor_scalar(out=out, in0=in0, scalar1=s1, scalar2=s2, op0=op0, op1=op1), r=r, w=w)

    def stt(out, in0, scalar, in1, op0, op1, r, w):
        return S.op("dve", lambda e: e.scalar_tensor_tensor(out=out, in0=in0, scalar=scalar, in1=in1, op0=op0, op1=op1), r=r, w=w)

    def memset(ap, val, w, eng="dve"):
        return S.op(eng, lambda e: e.memset(ap, val), w=w)

    def recip(out, in_, r, w):
        return S.op("dve", lambda e: e.reciprocal(out=out, in_=in_), r=r, w=w)

    def xk(c, ti):
        return ("X", c, ti)

    def tidx(t0):
        return t0 // 512

    S.dma("sp", pvec[:], pvec_d, w=["pvec"])
    S.dma("sp", ident[:], ident_d, w=["ident"])
    S.op("dve", lambda e: e.tensor_copy(out=identb[:], in_=ident[:]), r=["ident"], w=["identb"])
    memset(onesb[:], 1.0, w=["onesb"])

    with contextlib.ExitStack() as ph:
        xt = [sb(ph, f"xt{i}", [128, D]) for i in range(2)]
        scond = sb(ph, "scond", [128, 8, 3])
        mwb = [sb(ph, f"mwb{i}", [128, 8 * 1152]) for i in range(2)]
        mres = sb(ph, "mres", [128, 4, 9, 3])
        mall = sb(ph, "mall", [128, 4, 108])
        act(scond[:], pvec[:, PV_COND:PV_COND + 24].rearrange("p (k j) -> p k j", j=3), AF.Silu, r=["pvec"], w=["scond"])
        for q4 in range(4):
            S.dma("sp", mwb[q4 % 2][:], modw_d[q4], w=[("mwb", q4 % 2)])
            wv = mwb[q4 % 2][:].rearrange("p (k n) -> p k n", k=8)
            for cc in range(9):
                col = (q4 * 9 + cc) * 3
                for k in range(8):
                    mm(PS[2][:, col:col + 3], wv[:, k, cc * 128:(cc + 1) * 128], scond[:, k, :],
                       start=(k == 0), stop=(k == 7), r=[("mwb", q4 % 2), "scond"], w=[pk(2)], inc=(k == 7))
        for q4 in range(4):
            for j in range(3):
                tt(mres[:, q4, :, j], PS[2][:, q4 * 27:(q4 + 1) * 27].rearrange("p (c j) -> p c j", j=3)[:, :, j],
                   pvec[:, PV_MODB + q4 * 9: PV_MODB + (q4 + 1) * 9], ALU.add, r=[pk(2), "pvec"], w=[("mres", q4, j)])
        tk_ = S.dma("pool", mg_in, mres[:].rearrange("p q c j -> p (q c j)"), r=[("mres", q4, j) for q4 in range(4) for j in range(3)])
        S.wait("pool", tk_)
        nc.gpsimd.collective_compute("AllGather", ALU.bypass, replica_groups=[[0, 1, 2, 3], [4, 5, 6, 7]],
                                     ins=[mg_in], outs=[mg_out]).then_inc(S.ccsem2)
        nblk = 16
        it = 0
        for tb in range(nblk + 1):
            rows = 128 if tb < nblk else 16
            buf = xt[tb % 2]
            S.dma("sp", buf[0:rows, :], xin[tb * 128: tb * 128 + rows, :], w=[("xt", tb % 2)])
            for c in range(8):
                bank = it % 2
                it += 1
                S.op("pe", lambda e, c=c, bank=bank: e.transpose(out=PS[bank][:, 0:rows], in_=buf[0:rows, c * 128:(c + 1) * 128],
                                                           identity=ident[0:rows, 0:rows]),
                     r=[("xt", tb % 2), "ident"], w=[pk(bank)])
                dst = X[:, c, tb * 128: tb * 128 + rows]
                ti = tidx(tb * 128)
                if c % 2 == 0:
                    act(dst, PS[bank][:, 0:rows], AF.Copy, r=[pk(bank)], w=[("Xld", c, tb)])
                else:
                    ts(dst, PS[bank][:, 0:rows], 1.0, ALU.mult, r=[pk(bank)], w=[("Xld", c, tb)])
        S.wait("pool", (S.ccsem2, 1, None))
        S.dma("pool", mall[:], mg_out.rearrange("(r p) n -> p r n", p=128), w=["mall"])
        for l in range(2):
            for sl in range(2):
                q4 = sl * 2 + l
                mv = mall[:, :, q4 * 27:(q4 + 1) * 27].rearrange("p r (c j) -> p r c j", j=3)
                o0 = mods[:, l, sl * 36:(sl + 1) * 36, 0].rearrange("p (r c) -> p r c", r=4)
                o1 = mods[:, l, sl * 36:(sl + 1) * 36, 1].rearrange("p (r c) -> p r c", r=4)
                ts(o0, mv[:, :, :, 0], 1.0, ALU.mult, r=["mall"], w=[("mods", l, 0, sl)])
                ts(o1, mv[:, :, :, 1], pvec[:, PV_SELB:PV_SELB + 1], ALU.mult, r=["mall", "pvec"], w=[("mods", l, 1, sl)])
                stt(o1, mv[:, :, :, 2], pvec[:, PV_SELB + 1:PV_SELB + 2], o1, ALU.mult, ALU.add,
                    r=["mall", "pvec", ("mods", l, 1, sl)], w=[("mods", l, 1, sl)])
        for l in range(2):
            for n in range(3):
                for j in range(2):
                    stt(gsv[:, l, n, :, j], mods[:, l, (1 + 3 * n) * 8:(2 + 3 * n) * 8, j], 1.0,
                        pvec[:, PV_GAIN + (l * 3 + n) * 8: PV_GAIN + (l * 3 + n + 1) * 8], ALU.add, ALU.mult,
                        r=[("mods", l, j, 0), ("mods", l, j, 1), "pvec"], w=[("gsv", l, n, j)])
            for wi in range(2):
                for j in range(2):
                    i0 = 2 if wi == 0 else 8
                    ts(ghv[:, l, wi, :, j], mods[:, l, i0 * 8:(i0 + 1) * 8, j], 0.5, ALU.mult,
                       r=[("mods", l, j, 0), ("mods", l, j, 1)], w=[("ghv", l, wi, j)])
        S.barrier()

    def modcol(l, i, c, j):
        return mods[:, l, i * 8 + c, j:j + 1]

    def modnorm(stack_bufs, t0, tn, cj, l, n, ishift, out, okey):
        sq, rt, rstd, tmp = stack_bufs
        ti = tidx(t0)
        for c in range(8):
            act(sq[:, c, 0:tn], X[:, c, t0:t0 + tn], AF.Square, r=[xk(c, ti)], w=[("sq", c)])
        for c in range(8):
            mm(PS[6][:, 0:tn], onesb[:], sq[:, c, 0:tn], start=(c == 0), stop=(c == 7),
               r=[("sq", c), "onesb"], w=[pk(6)], inc=(c == 7))
        act(rt[:, 0:tn], PS[6][:, 0:tn], AF.Ln, r=[pk(6)], w=["rt"], bias=epsc[:, 0:1], scale=1.0 / D)
        act(rstd[:, 0:tn], rt[:, 0:tn], AF.Exp, r=["rt"], w=["rstd"], scale=-0.5)
        for c in range(8):
            tb_ = tmp[c % 2]
            stt(tb_[:, 0:tn], X[:, c, t0:t0 + tn], gsv[:, l, n, c, cj:cj + 1], rstd[:, 0:tn], ALU.mult, ALU.mult,
                r=[xk(c, ti), "rstd"], w=[("tmp", c % 2)])
            act(out(c), tb_[:, 0:tn], AF.Identity, r=[("tmp", c % 2)], w=[okey(c)], bias=modcol(l, ishift, c, cj))

    epsc = sb(es, "epsc", [128, 1])
    memset(epsc[:], EPS, w=["epsc"])
    S.barrier()

    def dbg_dump(idx):
        if debug:
            for c in range(8):
                S.dma("sp", dbg_d[idx, :, c * 256:(c + 1) * 256].rearrange("p (a b) -> p a b", a=1),
                      X[:, c:c + 1, 0:256], r=[xk(c, 0)])
            S.barrier()

    def ffn(l, wi, tiles):
        n = 0 if wi == 0 else 2
        ishift = 0 if wi == 0 else 6
        with contextlib.ExitStack() as ph:
            xn = sb(ph, "xn", [128, 8, NTOK], BF16)
            hb = sb(ph, "hb", [128, 6, NTOK], BF16)
            win = [sb(ph, f"win{i}", [128, 8, 256], BF16) for i in range(3)]
            wo = [sb(ph, f"wo{i}", [128, 6, 1024], BF16) for i in range(2)]
            sq = sb(ph, "sq", [128, 8, 512], BF16)
            rt = sb(ph, "rt", [128, 512])
            rstd = sb(ph, "rstd", [128, 512])
            tmp = [sb(ph, f"tmp{i}", [128, 512]) for i in range(2)]
            sl = [sb(ph, f"sl{i}", [128, 512], BF16) for i in range(2)]
            itc = 0
            for g, chunks in enumerate(GROUPS):
                for jj, j in enumerate(chunks):
                    slot = j % 3
                    S.dma("pool", win[slot][:].rearrange("p k n -> p (k n)"), w1in_d[l, wi, j], w=[("win", slot)])
                    for (t0, tn, cj) in tiles:
                        ti = tidx(t0)
                        if j == 0:
                            modnorm((sq, rt, rstd, tmp), t0, tn, cj, l, n, ishift,
                                    out=lambda c, t0=t0, tn=tn: xn[:, c, t0:t0 + tn], okey=lambda c, ti=ti: ("xn", c, ti))
                        par = itc % 2
                        itc += 1
                        pa, pb = PS[par], PS[2 + par]
                        for k in range(8):
                            mm(pa[:, 0:tn], win[slot][:, k, 0:128], xn[:, k, t0:t0 + tn], start=(k == 0), stop=(k == 7),
                               r=[("win", slot), ("xn", k, ti)], w=[pk(par)], inc=(k == 7))
                        for k in range(8):
                            mm(pb[:, 0:tn], win[slot][:, k, 128:256], xn[:, k, t0:t0 + tn], start=(k == 0), stop=(k == 7),
                               r=[("win", slot), ("xn", k, ti)], w=[pk(2 + par)], inc=(k == 7))
                        act(sl[par][:, 0:tn], pa[:, 0:tn], AF.Silu, r=[pk(par)], w=[("sl", par)])
                        tt(hb[:, jj, t0:t0 + tn], sl[par][:, 0:tn], pb[:, 0:tn], ALU.mult,
                           r=[("sl", par), pk(2 + par)], w=[("hb", jj, ti)])
                ws = g % 2
                nj = len(chunks)
                S.dma("pool", wo[ws][:, 0:nj, :], w1out_d[l, wi, chunks[0]:chunks[0] + nj].rearrange("j p n -> p j n"),
                      w=[("wo", ws)])
                ito = 0
                for (t0, tn, cj) in tiles:
                    ti = tidx(t0)
                    for m in range(8):
                        par = ito % 2
                        ito += 1
                        po = PS[4 + par]
                        for jj in range(nj):
                            mm(po[:, 0:tn], wo[ws][:, jj, m * 128:(m + 1) * 128], hb[:, jj, t0:t0 + tn],
                               start=(jj == 0), stop=(jj == nj - 1), r=[("wo", ws), ("hb", jj, ti)], w=[pk(4 + par)],
                               inc=(jj == nj - 1))
                        stt(X[:, m, t0:t0 + tn], po[:, 0:tn], ghv[:, l, wi, m, cj:cj + 1], X[:, m, t0:t0 + tn],
                            ALU.mult, ALU.add, r=[pk(4 + par), xk(m, ti)], w=[xk(m, ti)])
            S.barrier()

    def even_mixer():
        l = 0
        with contextlib.ExitStack() as ph:
            xn = sb(ph, "exn", [128, 8, 1040], BF16)
            yb = sb(ph, "eyb", [128, 8, 1024], BF16)
            ews = [sb(ph, f"ews{i}", [128, 8, 128], BF16) for i in range(3)]
            mixw = sb(ph, "mixw_s", [128, 8, 1024], BF16)
            poolw = sb(ph, "poolw_s", [128, 4, 128], BF16)
            Ul = [sb(ph, f"U{i}", [128, 1088]) for i in range(2)]
            U2l = [sb(ph, f"U2{i}", [128, 1088]) for i in range(2)]
            Pal = [sb(ph, "Pa0", [128, 1088])] * 2
            Pbl = [sb(ph, "Pb0", [128, 1088])] * 2
            GBl = [sb(ph, f"GB{i}", [128, 1024]) for i in range(2)]
            Acl = [sb(ph, "Ac0", [128, 1024])] * 2
            dbfl = [sb(ph, "dbf0", [128, 1024], BF16)] * 2
            cix = [0]
            icc = sb(ph, "icc_s", [128, 4, 256])
            icl = sb(ph, "icl_s", [128, 4, 1024])
            S.dma("sp", icc[:].rearrange("p g t -> p (g t)"), icc_d, w=["icc"])
            S.dma("sp", icl[:].rearrange("p g t -> p (g t)"), icl_d, w=["icl"])
            S.dma("pool", mixw[:], mixw_d[0].rearrange("k p n -> p k n"), w=["mixw"])
            S.dma("pool", poolw[:], poolw_d.rearrange("g p n -> p g n"), w=["poolw"])
            eit = [0]
            esl = [0]
            for half in range(2):
                if half == 0:
                    tiles = [(0, 512, 0), (512, 512, 0)]
                    nseq, L, base = 4, 256, 0
                else:
                    tiles = [(1024, 512, 1), (1536, 512, 1), TILE_HALO]
                    nseq, L, base = 1, 1024, 1024
                Lp = L + 16
                cj = half

                def xoff(t0):
                    return (t0 - base) if t0 < 2048 else (1024 + t0 - 2048)

                with contextlib.ExitStack() as pn:
                    sq = sb(pn, "sq", [128, 8, 512], BF16)
                    rt = sb(pn, "rt", [128, 512])
                    rstd = sb(pn, "rstd", [128, 512])
                    tmp = [sb(pn, f"tmp{i}", [128, 512]) for i in range(2)]
                    for (t0, tn, _) in tiles:
                        ti = tidx(t0)
                        o = xoff(t0)
                        modnorm((sq, rt, rstd, tmp), t0, tn, cj, l, 1, 3,
                                out=lambda c, o=o, tn=tn: xn[:, c, o:o + tn], okey=lambda c, ti=ti: ("exn", c, ti))
                    S.barrier()

                def pview(buf):
                    return buf[:, 0:nseq * Lp].rearrange("p (s t) -> p s t", s=nseq)

                def project(cidx, dst_buf, dkey, padded):
                    slot = esl[0] % 3
                    esl[0] += 1
                    S.dma("pool", ews[slot][:], ewin_d[:, :, cidx * 128:(cidx + 1) * 128].rearrange("k p n -> p k n"),
                          w=[("ews", slot)])
                    for (t0, tn, _) in tiles:
                        ti = tidx(t0)
                        o = xoff(t0)
                        bank = eit[0] % 2
                        eit[0] += 1
                        for k in range(8):
                            mm(PS[bank][:, 0:tn], ews[slot][:, k, :], xn[:, k, o:o + tn], start=(k == 0), stop=(k == 7),
                               r=[("ews", slot), ("exn", k, ti)], w=[pk(bank)], inc=(k == 7))
                        if t0 >= 2048:
                            if padded:
                                tt(dst_buf[:, 0:8], PS[bank][:, 0:8], pvec[:, PV_HMASK:PV_HMASK + 8], ALU.mult,
                                   r=[pk(bank), "pvec"], w=[dkey])
                                tt(dst_buf[:, 8 + L:16 + L], PS[bank][:, 8:16], pvec[:, PV_HMASK + 8:PV_HMASK + 16], ALU.mult,
                                   r=[pk(bank), "pvec"], w=[dkey])
                            continue
                        if padded:
                            if half == 0:
                                s0 = (t0 // 256)
                                dst = pview(dst_buf)[:, s0:s0 + 2, 8:8 + 256]
                                src = PS[bank][:, 0:512].rearrange("p (s t) -> p s t", s=2)
                            else:
                                dst = dst_buf[:, 8 + o:8 + o + tn]
                                src = PS[bank][:, 0:tn]
                        else:
                            dst = dst_buf[:, o:o + tn]
                            src = PS[bank][:, 0:tn]
                        act(dst, src, AF.Copy, r=[pk(bank)], w=[dkey])

                for pp in range(2):
                    memset(Ul[pp][:], 0.0, w=[("U", pp)])
                    memset(U2l[pp][:], 0.0, w=[("U2", pp)], eng="pool" if False else "dve")
                W_ = [2, 4, 8, 16]
                for g in range(4):
                    par = cix[0] % 2
                    cix[0] += 1
                    U, Pa, Pb, dbf = Ul[par], Pal[par], Pbl[par], dbfl[par]
                    kU, kPa, kPb, kd = ("U", par), ("Pa", 0), ("Pb", 0), ("dbf", 0)
                    project(g, U, kU, True)
                    Uv, Pav, Pbv = pview(U), pview(Pa), pview(Pb)
                    tt(Pav[:, :, 0:Lp - 1], Uv[:, :, 0:Lp - 1], Uv[:, :, 1:Lp], ALU.add, r=[kU], w=[kPa])
                    cur, curk, ln, step = Pav, kPa, Lp - 1, 2
                    oth, othk = Pbv, kPb
                    while step < W_[g]:
                        tt(oth[:, :, 0:ln - step], cur[:, :, 0:ln - step], cur[:, :, step:ln], ALU.add, r=[curk], w=[othk])
                        cur, oth, curk, othk = oth, cur, othk, curk
                        ln -= step
                        step *= 2
                    hw = W_[g] // 2
                    icv = (icc[:, g, :].rearrange("p (s t) -> p s t", s=1) if half == 0 else None)
                    for s_ in range(nseq):
                        ic = icc[:, g, :] if half == 0 else icl[:, g, :]
                        tt(oth[:, s_, 0:L], cur[:, s_, 8 - hw:8 - hw + L], ic, ALU.mult, r=[curk, "icc", "icl"], w=[othk])
                    dv = dbf[:, 0:nseq * L].rearrange("p (s t) -> p s t", s=nseq)
                    tt(dv, oth[:, :, 0:L], Uv[:, :, 8:8 + L], ALU.subtract, r=[othk, kU], w=[kd])
                    for (t0, tn, _) in tiles:
                        if t0 >= 2048:
                            continue
                        o = xoff(t0)
                        bank = eit[0] % 2
                        eit[0] += 1
                        mm(PS[bank][:, 0:tn], poolw[:, g, :], dbf[:, o:o + tn], start=True, stop=True,
                           r=["poolw", kd], w=[pk(bank)], inc=True)
                        act(yb[:, g, o:o + tn], PS[bank][:, 0:tn], AF.Copy, r=[pk(bank), "pvec"], w=[("eyb", g)],
                            scale=pvec[:, PV_PSC + g:PV_PSC + g + 1])
                for i in range(4):
                    par = cix[0] % 2
                    cix[0] += 1
                    U, U2, Pa, GB, Ac = Ul[par], U2l[par], Pal[par], GBl[par], Acl[par]
                    kU, kU2, kPa, kGB, kAc = ("U", par), ("U2", par), ("Pa", 0), ("GB", par), ("Ac", 0)
                    project(4 + i, U, kU, True)
                    project(12 + i, U2, kU2, True)
                    project(8 + i, GB, kGB, False)
                    tt(Pa[:, 0:nseq * Lp], U[:, 0:nseq * Lp], U2[:, 0:nseq * Lp], ALU.mult, r=[kU, kU2], w=[kPa])
                    Zv = pview(Pa)
                    Av = Ac[:, 0:nseq * L].rearrange("p (s t) -> p s t", s=nseq)
                    cw = lambda jx: pvec[:, PV_CONV + jx * 4 + i: PV_CONV + jx * 4 + i + 1]
                    act(Av, Zv[:, :, 8:8 + L], AF.Copy, r=[kPa, "pvec"], w=[kAc], scale=cw(1))
                    for s_ in range(nseq):
                        stt(Av[:, s_, :], Zv[:, s_, 7:7 + L], cw(0), Av[:, s_, :], ALU.mult, ALU.add, r=[kPa, kAc], w=[kAc])
                        stt(Av[:, s_, :], Zv[:, s_, 9:9 + L], cw(2), Av[:, s_, :], ALU.mult, ALU.add, r=[kPa, kAc], w=[kAc])
                    tt(yb[:, 4 + i, 0:1024], GB[:, 0:1024], Ac[:, 0:1024], ALU.mult, r=[kGB, kAc], w=[("eyb", 4 + i)])
                for (t0, tn, _) in tiles:
                    if t0 >= 2048:
                        continue
                    ti = tidx(t0)
                    o = xoff(t0)
                    for m in range(8):
                        bank = 4 + eit[0] % 2
                        eit[0] += 1
                        for k in range(8):
                            mm(PS[bank][:, 0:tn], mixw[:, k, m * 128:(m + 1) * 128], yb[:, k, o:o + tn],
                               start=(k == 0), stop=(k == 7), r=["mixw", ("eyb", k)], w=[pk(bank)], inc=(k == 7))
                        stt(X[:, m, t0:t0 + tn], PS[bank][:, 0:tn], modcol(l, 5, m, cj), X[:, m, t0:t0 + tn],
                            ALU.mult, ALU.add, r=[pk(bank), xk(m, ti)], w=[xk(m, ti)])
            S.barrier()

    def odd_mixer():
        l = 1
        lam_init = 0.8 - 0.6 * math.exp(-0.3 * 1)
        with contextlib.ExitStack() as ph:
            QTl = sb(ph, "QTl", [128, 8, 1024], BF16)
            bvec = sb(ph, "bvec_s", [128, BV_N])
            lamc = sb(ph, "lamc", [128, 4])
            gdn = sb(ph, "gdn", [128, 128])
            small = sb(ph, "small", [128, 8])
            o1s = sb(ph, "o1s", [128, 4, 129])
            dtmp = sb(ph, "dtmp", [128, 128])
            dtmp2 = sb(ph, "dtmp2", [128, 128])
            lt = sb(ph, "lt", [128, 128])
            S.dma("sp", bvec[:], bvec_d, w=["bvec"])
            bl = bvec[:, BV_LAM:BV_LAM + 256].rearrange("p (a d) -> p a d", a=4)
            tt(lt[:, 0:64], bl[:, 0, :], bl[:, 1, :], ALU.mult, r=["bvec"], w=["lt0"])
            tt(lt[:, 64:128], bl[:, 2, :], bl[:, 3, :], ALU.mult, r=["bvec"], w=["lt1"])
            S.op("dve", lambda e: e.reduce_sum(out=lamc[:, 0:2], in_=lt[:].rearrange("p (a d) -> p a d", a=2),
                                               axis=mybir.AxisListType.X), r=["lt0", "lt1"], w=["lamc"])
            act(lamc[:, 0:2], lamc[:, 0:2], AF.Exp, r=["lamc"], w=["lamc"])
            tt(lamc[:, 2:3], lamc[:, 1:2], lamc[:, 0:1], ALU.subtract, r=["lamc"], w=["lamc"])
            ts(lamc[:, 2:3], lamc[:, 2:3], -lam_init, ALU.add, r=["lamc"], w=["lamc"])
            ts(gdn[:], bvec[:, BV_DN:BV_DN + 128], 1.0 - lam_init, ALU.mult, r=["bvec"], w=["gdn"])

            def attn_norm_nat(psb, dst, dkey, rkeys):
                recip(small[:, 0:1], psb[:, 64:65], r=rkeys, w=["small0"])
                ts(dst, psb[:, 0:64], small[:, 0:1], ALU.mult, r=rkeys + ["small0"], w=[dkey])

            def attn_norm_diff(o1, o2ps, dst, dkey, rkeys):
                recip(small[:, 1:2], o1[:, 128:129], r=rkeys, w=["small1"])
                recip(small[:, 2:3], o2ps[:, 128:129], r=rkeys, w=["small2"])
                tt(small[:, 2:3], small[:, 2:3], lamc[:, 2:3], ALU.mult, r=["small2", "lamc"], w=["small2"])
                ts(dtmp[:], o2ps[:, 0:128], small[:, 2:3], ALU.mult, r=rkeys + ["small2"], w=["dtmp"])
                stt(dtmp[:], o1[:, 0:128], small[:, 1:2], dtmp[:], ALU.mult, ALU.add, r=rkeys + ["small1", "dtmp"], w=["dtmp"])
                act(dtmp2[:], dtmp[:], AF.Square, r=["dtmp"], w=["dtmp2", "small3"], accum_out=small[:, 3:4])
                act(small[:, 4:5], small[:, 3:4], AF.Ln, r=["small3"], w=["small4"], bias=epsc[:, 0:1], scale=1.0 / 128)
                act(small[:, 5:6], small[:, 4:5], AF.Exp, r=["small4"], w=["small5"], scale=-0.5)
                stt(dst, dtmp[:], small[:, 5:6], gdn[:], ALU.mult, ALU.mult, r=["dtmp", "small5", "gdn"], w=[dkey])

            oit = [0]
            osl = [0]

            def pipeline(units):
                n_ = len(units)
                if n_:
                    units[0][0]()
                for i_ in range(n_):
                    if i_ + 1 < n_:
                        units[i_ + 1][0]()
                    units[i_][1]()
                    units[i_][2]()

            def out_project(p, otok, nblk, t0, cj):
                ntok = nblk * 128
                oT = sb(p, "oT", [128, 8, ntok], BF16)
                mixw = sb(p, "mixw1", [128, 8, 1024], BF16)
                S.dma("pool", mixw[:], mixw_d[1].rearrange("k p n -> p k n"), w=["mixw1"])
                trn = 0
                for tb in range(nblk):
                    for c in range(8):
                        bank = 2 + trn % 2
                        trn += 1
                        pbf = PS[bank][:].bitcast(BF16)
                        S.op("pe", lambda e: e.transpose(out=pbf[:, 0:128], in_=otok[:, tb, c * 128:(c + 1) * 128], identity=identb[:]),
                             r=[("otok", tb), "identb"], w=[pk(bank)])
                        act(oT[:, c, tb * 128:(tb + 1) * 128], pbf[:, 0:128], AF.Copy, r=[pk(bank)], w=[("oT", c, tb // 4)])
                for tq in range(ntok // 512):
                    ti = tidx(t0 + tq * 512)
                    for m in range(8):
                        bank = oit[0] % 2
                        oit[0] += 1
                        for k in range(8):
                            mm(PS[bank][:, :], mixw[:, k, m * 128:(m + 1) * 128], oT[:, k, tq * 512:(tq + 1) * 512],
                               start=(k == 0), stop=(k == 7), r=["mixw1", ("oT", k, tq)], w=[pk(bank)], inc=(k == 7))
                        stt(X[:, m, t0 + tq * 512:t0 + (tq + 1) * 512], PS[bank][:, :], modcol(l, 5, m, cj),
                            X[:, m, t0 + tq * 512:t0 + (tq + 1) * 512], ALU.mult, ALU.add, r=[pk(bank), xk(m, ti)], w=[xk(m, ti)])

            def norm_bufs(p):
                return (sb(p, "sq", [128, 8, 512], BF16), sb(p, "rt", [128, 512]), sb(p, "rstd", [128, 512]),
                        [sb(p, f"tmp{i}", [128, 512]) for i in range(2)])

            with contextlib.ExitStack() as p1:
                xn = sb(p1, "oxn", [128, 8, 1024], BF16)
                nb = norm_bufs(p1)
                tmp = nb[3]
                ows = [sb(p1, f"ows{i}", [128, 8, 512], BF16) for i in range(3)]
                stgb = [sb(p1, f"stgb{i}", [128, 1024], BF16) for i in range(2)]
                rc_t = sb(p1, "rc_t", [128, 1024])
                rs_t = sb(p1, "rs_t", [128, 1024])
                rtmp = sb(p1, "rtmp", [128, 512])
                S.dma("sp", rc_t[:], ropec_d, w=["rc_t"])
                S.dma("sp", rs_t[:], ropes_d, w=["rs_t"])
                for tq in range(2):
                    modnorm(nb, 1024 + tq * 512, 512, 1, l, 1, 3,
                            out=lambda c, tq=tq: xn[:, c, tq * 512:(tq + 1) * 512], okey=lambda c, tq=tq: ("oxn", c, tq))

                def load_slab(si):
                    slot = osl[0] % 3
                    osl[0] += 1
                    S.dma("pool", ows[slot][:], owin_d[:, :, si * 512:(si + 1) * 512].rearrange("k p n -> p k n"),
                          w=[("ows", slot)])
                    return slot

                gin_keys = []

                def put_gin(row0, src_ap, rkeys):
                    gin_keys.append(("gin", row0))
                    sec_ = row0 // 128
                    S.dma("sp", gin_b[sec_ // 4][(sec_ % 4) * 128:(sec_ % 4 + 1) * 128, :], src_ap, r=rkeys, w=[("gin", row0)])

                def proj_fm(slot, consume):
                    for c in range(4):
                        for tq in range(2):
                            bank = oit[0] % 2
                            oit[0] += 1
                            for k in range(8):
                                mm(PS[bank][:, :], ows[slot][:, k, c * 128:(c + 1) * 128], xn[:, k, tq * 512:(tq + 1) * 512],
                                   start=(k == 0), stop=(k == 7), r=[("ows", slot), ("oxn", k, tq)], w=[pk(bank)], inc=(k == 7))
                            consume(c, tq, bank)

                sl_ = load_slab(0)
                proj_fm(sl_, lambda c, tq, bank: act(QTl[:, c, tq * 512:(tq + 1) * 512], PS[bank][:, :], AF.Copy,
                                                     r=[pk(bank)], w=[("QTl", c, tq)]))
                sl_ = load_slab(1)

                def cons_nk(c, tq, bank):
                    sbuf = stgb[c % 2]
                    act(sbuf[:, tq * 512:(tq + 1) * 512], PS[bank][:, :], AF.Copy, r=[pk(bank)], w=[("stgb", c % 2, tq)])
                    if tq == 1:
                        put_gin(c * 128, sbuf[:], [("stgb", c % 2, 0), ("stgb", c % 2, 1)])
                proj_fm(sl_, cons_nk)
                for (si, sp_, kind) in ((3, 6, "q"), (4, 7, "k")):
                    sA = load_slab(si)
                    sB = load_slab(sp_)
                    for c in range(4):
                        for tq in range(2):
                            b1 = oit[0] % 2
                            b2 = 2 + oit[0] % 2
                            oit[0] += 1
                            for (bk, sl2) in ((b1, sA), (b2, sB)):
                                for k in range(8):
                                    mm(PS[bk][:, :], ows[sl2][:, k, c * 128:(c + 1) * 128], xn[:, k, tq * 512:(tq + 1) * 512],
                                       start=(k == 0), stop=(k == 7), r=[("ows", sl2), ("oxn", k, tq)], w=[pk(bk)], inc=(k == 7))
                            tt(rtmp[:], PS[b1][:, :], rc_t[:, tq * 512:(tq + 1) * 512], ALU.mult, r=[pk(b1), "rc_t"], w=["rtmp"])
                            tt(tmp[0][:], PS[b2][:, :], rs_t[:, tq * 512:(tq + 1) * 512], ALU.mult, r=[pk(b2), "rs_t"], w=[("tmp", 0)])
                            if kind == "q":
                                tt(QTl[:, 4 + c, tq * 512:(tq + 1) * 512], rtmp[:], tmp[0][:], ALU.add,
                                   r=["rtmp", ("tmp", 0)], w=[("QTl", 4 + c, tq)])
                            else:
                                sbuf = stgb[c % 2]
                                tt(sbuf[:, tq * 512:(tq + 1) * 512], rtmp[:], tmp[0][:], ALU.add,
                                   r=["rtmp", ("tmp", 0)], w=[("stgb", c % 2, tq)])
                                if tq == 1:
                                    put_gin((4 + c) * 128, sbuf[:], [("stgb", c % 2, 0), ("stgb", c % 2, 1)])
                for (si, sec0) in ((2, 8), (5, 12)):
                    sl_ = load_slab(si)
                    for tb in range(8):
                        bank = 2 + oit[0] % 2
                        oit[0] += 1
                        for k in range(8):
                            mm(PS[bank][:, :], xn[:, k, tb * 128:(tb + 1) * 128], ows[sl_][:, k, :],
                               start=(k == 0), stop=(k == 7), r=[("ows", sl_), ("oxn", k, tb // 4)], w=[pk(bank)], inc=(k == 7))
                        sbuf = stgb[(tb // 2) % 2]
                        act(sbuf[:, (tb % 2) * 512:(tb % 2 + 1) * 512], PS[bank][:, :], AF.Copy, r=[pk(bank)],
                            w=[("stgb", (tb // 2) % 2, tb % 2)])
                        if tb % 2 == 1:
                            put_gin((sec0 + tb // 2) * 128, sbuf[:], [("stgb", (tb // 2) % 2, 0), ("stgb", (tb // 2) % 2, 1)])
                for t_ in S._deps(gin_keys, []):
                    S.wait("pool", t_)
                S.barrier()
                if True:
                    for ci in range(4):
                        nc.gpsimd.collective_compute("AllGather", ALU.bypass, replica_groups=[[0, 1, 2, 3], [4, 5, 6, 7]],
                                                     ins=[gin_f[ci]], outs=[gout_f[ci]]).then_inc(S.ccsem)


            if CUT < 1.1:
                raise _Cut()
            sit = [0]
            for hc in range(2):
                with contextlib.ExitStack() as p1:
                    t0 = hc * 512
                    xn = sb(p1, "cxn", [128, 8, 512], BF16)
                    nb = norm_bufs(p1)
                    ows = [sb(p1, f"cows{i}", [128, 8, 512], BF16) for i in range(2)]
                    QT = sb(p1, "QT", [128, 8, 512], BF16)
                    KT = sb(p1, "KT", [128, 8, 512], BF16)
                    VN = sb(p1, "VN", [128, 4, 8, 65], BF16)
                    VD = sb(p1, "VD", [128, 4, 4, 129], BF16)
                    stage = [sb(p1, f"stg{i}", [128, 512]) for i in range(2)]
                    ET = [sb(p1, f"ET{i}", [128, 256], BF16) for i in range(4)]
                    otok = sb(p1, "otok", [128, 4, 1024], BF16)
                    memset(VN[:, :, :, 64:65], 1.0, w=["VNones"])
                    memset(VD[:, :, :, 128:129], 1.0, w=["VDones"])
                    modnorm(nb, t0, 512, 0, l, 1, 3, out=lambda c: xn[:, c, :], okey=lambda c: ("cxn", c))

                    def load_slab_c(si):
                        slot = osl[0] % 2
                        osl[0] += 1
                        S.dma("pool", ows[slot][:], owin_d[:, :, si * 512:(si + 1) * 512].rearrange("k p n -> p k n"),
                              w=[("cows", slot)])
                        return slot

                    if CUT < 1.12:
                        raise _Cut()
                    for (si, isq, ch0) in ((0, True, 0), (1, False, 0), (3, True, 4), (4, False, 4)):
                        slot = load_slab_c(si)
                        for c in range(4):
                            bank = oit[0] % 2
                            oit[0] += 1
                            for k in range(8):
                                mm(PS[bank][:, :], ows[slot][:, k, c * 128:(c + 1) * 128], xn[:, k, :],
                                   start=(k == 0), stop=(k == 7), r=[("cows", slot), ("cxn", k)], w=[pk(bank)], inc=(k == 7))
                            dstT = QT if isq else KT
                            act(dstT[:, ch0 + c, :], PS[bank][:, :], AF.Copy, r=[pk(bank)], w=[("QT" if isq else "KT", ch0 + c)])
                    if CUT < 1.14:
                        raise _Cut()
                    for (si, od, vdst) in ((1, onk_d, None), (2, onv_d, "n"), (4, odk_d, None), (5, odv_d, "d")):
                        if CUT < 1.16 and si == 2:
                            raise _Cut()
                        slot = load_slab_c(si)
                        for tb in range(4):
                            bank = 2 + oit[0] % 2
                            oit[0] += 1
                            for k in range(8):
                                mm(PS[bank][:, :], xn[:, k, tb * 128:(tb + 1) * 128], ows[slot][:, k, :],
                                   start=(k == 0), stop=(k == 7), r=[("cows", slot), ("cxn", k)], w=[pk(bank)], inc=(k == 7))
                            sx = sit[0] % 2
                            sit[0] += 1
                            act(stage[sx][:], PS[bank][:, :], AF.Copy, r=[pk(bank)], w=[("stage", sx)])
                            if True:
                                tk = S.dma("sp", od[t0 + tb * 128:t0 + (tb + 1) * 128, :], stage[sx][:], r=[("stage", sx)])
                                S.out_toks.append(tk)
                            if vdst == "n":
                                act(VN[:, tb, :, 0:64], PS[bank][:, :].rearrange("p (h d) -> p h d", h=8), AF.Copy,
                                    r=[pk(bank)], w=[("VN", tb)])
                            elif vdst == "d":
                                act(VD[:, tb, :, 0:128], PS[bank][:, :].rearrange("p (h d) -> p h d", h=4), AF.Copy,
                                    r=[pk(bank)], w=[("VD", tb)])
                    units = []
                    ui = 0
                    for s_ in range(2):
                        q0 = s_ * 256
                        for h in range(8):
                            def A(s_=s_, h=h, q0=q0, ui=ui):
                                pb0 = 64 * (h % 2)
                                for kb in range(2):
                                    bank = (ui % 2) * 2 + kb
                                    mm(PS[bank][:, 0:256], KT[pb0:pb0 + 64, h // 2, q0 + kb * 128:q0 + (kb + 1) * 128],
                                       QT[pb0:pb0 + 64, h // 2, q0:q0 + 256], start=True, stop=True,
                                       r=[("KT", h // 2), ("QT", h // 2)], w=[pk(bank)], inc=True)

                            def B(ui=ui):
                                for kb in range(2):
                                    bank = (ui % 2) * 2 + kb
                                    ei = (ui % 2) * 2 + kb
                                    act(ET[ei][:], PS[bank][:, 0:256], AF.Exp, r=[pk(bank)], w=[("ET", ei)], scale=0.125)

                            def C(s_=s_, h=h, ui=ui):
                                for qb in range(2):
                                    pbk = 4 + (h * 2 + qb) % 4
                                    for kb in range(2):
                                        ei = (ui % 2) * 2 + kb
                                        mm(PS[pbk][:, 0:65], ET[ei][:, qb * 128:(qb + 1) * 128], VN[:, s_ * 2 + kb, h, :],
                                           start=(kb == 0), stop=(kb == 1), r=[("ET", ei), ("VN", s_ * 2 + kb), "VNones"],
                                           w=[pk(pbk)], inc=(kb == 1))
                                    attn_norm_nat(PS[pbk], otok[:, s_ * 2 + qb, h * 64:(h + 1) * 64], ("otok", s_ * 2 + qb), [pk(pbk)])
                            units.append((A, B, C))
                            ui += 1
                        for h in range(4):
                            for j in range(2):
                                def A(s_=s_, h=h, j=j, q0=q0, ui=ui):
                                    pb0 = 64 * j
                                    for kb in range(2):
                                        bank = (ui % 2) * 2 + kb
                                        mm(PS[bank][:, 0:256], KT[pb0:pb0 + 64, 4 + h, q0 + kb * 128:q0 + (kb + 1) * 128],
                                           QT[pb0:pb0 + 64, 4 + h, q0:q0 + 256], start=True, stop=True,
                                           r=[("KT", 4 + h), ("QT", 4 + h)], w=[pk(bank)], inc=True)

                                def B(ui=ui):
                                    for kb in range(2):
                                        bank = (ui % 2) * 2 + kb
                                        ei = (ui % 2) * 2 + kb
                                        act(ET[ei][:], PS[bank][:, 0:256], AF.Exp, r=[pk(bank)], w=[("ET", ei)], scale=0.125)

                                def C(s_=s_, h=h, j=j, ui=ui):
                                    for qb in range(2):
                                        pbk = 4 + 2 * qb + j
                                        for kb in range(2):
                                            ei = (ui % 2) * 2 + kb
                                            mm(PS[pbk][:, 0:129], ET[ei][:, qb * 128:(qb + 1) * 128], VD[:, s_ * 2 + kb, h, :],
                                               start=(kb == 0), stop=(kb == 1), r=[("ET", ei), ("VD", s_ * 2 + kb), "VDones"],
                                               w=[pk(pbk)], inc=(kb == 1))
                                    if j == 1:
                                        for qb in range(2):
                                            pj0 = 4 + 2 * qb
                                            act(o1s[:, qb, :], PS[pj0][:, 0:129], AF.Copy, r=[pk(pj0)], w=[("o1s", qb)])
                                            attn_norm_diff(o1s[:, qb, :], PS[pj0 + 1][:, 0:129],
                                                           otok[:, s_ * 2 + qb, 512 + h * 128:512 + (h + 1) * 128], ("otok", s_ * 2 + qb),
                                                           [("o1s", qb), pk(pj0 + 1)])
                                units.append((A, B, C))
                                ui += 1
                    pipeline(units)
                    if CUT < 1.8:
                        raise _Cut()
                    out_project(p1, otok, 4, t0, 0)
                    S.barrier()

            if CUT < 3:
                raise _Cut()
            for q_ in ("sp", "pool"):
                S.wait(q_, (S.ccsem, 4, None))
            with contextlib.ExitStack() as pl:
                otok = sb(pl, "otokl", [128, 8, 1024], BF16)
                ETl = [sb(pl, f"ETl{i}", [128, 512], BF16) for i in range(3)]
                PTl = [sb(pl, f"PTl{i}", [128, 512], BF16) for i in range(3)]
                gviews = [g_.rearrange("(r s p) n -> r s p n", r=4, s=4) for g_ in gout_b]
                with contextlib.ExitStack() as p2:
                    KNw = sb(p2, "KNw", [128, 4, 1536], BF16)
                    VNw = sb(p2, "VNw", [128, 12, 8, 65], BF16)
                    KNc = sb(p2, "KNc", [128, 4, 512], BF16)
                    VNc = sb(p2, "VNc", [128, 4, 8, 65], BF16)
                    memset(VNw[:, :, :, 64:65], 1.0, w=["VNw1"])
                    memset(VNc[:, :, :, 64:65], 1.0, w=["VNc1"])
                    S.dma("sp", KNw[:, :, 256:1280], gin_b[0][0:512, :].rearrange("(c p) t -> p c t", p=128), w=["KNw_own"])
                    for sec in range(4):
                        for hf in range(2):
                            S.dma("sp", VNw[:, 2 + sec * 2 + hf, :, 0:64],
                                  gin_b[2][sec * 128:(sec + 1) * 128, hf * 512:(hf + 1) * 512].rearrange("p (h d) -> p h d", h=8),
                                  w=[("VNw_own", sec, hf)])
                    S.dma("pool", KNc[:], cnkT_d.rearrange("(c p) k -> p c k", p=128), w=["KNc"])
                    for kb in range(4):
                        S.dma("pool", VNc[:, kb, :, 0:64], cnv_d[kb * 128:(kb + 1) * 128, :].rearrange("p (h d) -> p h d", h=8),
                              w=[("VNc", kb)])
                    with contextlib.ExitStack() as p2a:
                        Gk = [sb(p2a, f"Gk{i}", [128, 4, 4, 256], BF16) for i in range(2)]
                        Gv = [sb(p2a, f"Gv{i}", [128, 4, 1024], BF16) for i in range(2)]
                        for side in range(2):
                            cols = slice(768, 1024) if side == 0 else slice(0, 256)
                            for rr in range(4):
                                S.dma("sp", Gk[side][:, rr, :, :],
                                      gout_b[0][rr * 512:rr * 512 + 512, cols].rearrange("(c p) t -> p c t", p=128), w=[("Gk", side, rr)])
                            sec = 3 if side == 0 else 0
                            S.dma("sp", Gv[side][:], gviews[2][:, sec].rearrange("r p n -> p r n"), w=[("Gv", side)])
                            selo = PV_SELP if side == 0 else PV_SELN
                            kdst = KNw[:, :, 0:256] if side == 0 else KNw[:, :, 1280:1536]
                            kkey = ("KNw_h", side)
                            ts(kdst, Gk[side][:, 0, :, :], pvec[:, selo:selo + 1], ALU.mult, r=[("Gk", side, 0), "pvec"], w=[kkey])
                            for rr in range(1, 4):
                                stt(kdst, Gk[side][:, rr, :, :], pvec[:, selo + rr:selo + rr + 1], kdst, ALU.mult, ALU.add,
                                    r=[("Gk", side, rr), "pvec", kkey], w=[kkey])
                            for blk in range(2):
                                vdst = VNw[:, (0 if side == 0 else 10) + blk, :, 0:64]
                                vkey = ("VNw_h", side, blk)
                                src = lambda rr: Gv[side][:, rr, blk * 512:(blk + 1) * 512].rearrange("p (h d) -> p h d", h=8)
                                ts(vdst, src(0), pvec[:, selo:selo + 1], ALU.mult, r=[("Gv", side), "pvec"], w=[vkey])
                                for rr in range(1, 4):
                                    stt(vdst, src(rr), pvec[:, selo + rr:selo + rr + 1], vdst, ALU.mult, ALU.add,
                                        r=[("Gv", side), "pvec", vkey], w=[vkey])
                        S.barrier()
                    TB2 = sb(p2, "TB2", [128, 8, 18, 64], BF16)
                    tsrc = [sb(p2, f"tsrc{i}", [128, 15, 64]) for i in range(2)]
                    amat = sb(p2, "amat_s", [24, 1536], BF16)
                    bmat = sb(p2, "bmat_s", [24, 1024], BF16)
                    S.dma("pool", amat[:], amat_d, w=["amat"])
                    S.dma("pool", bmat[:], bmat_d, w=["bmat"])
                    memset(TB2[:], 0.0, w=["TB2z"])
                    cmb = bvec[:, BV_CM:BV_CM + 64].unsqueeze(1).broadcast_to([128, 15, 64])
                    cmb_lo = bvec[0:64, BV_CM:BV_CM + 64].unsqueeze(1).broadcast_to([64, 15, 64])
                    cmb_hi = bvec[64:128, BV_CM:BV_CM + 64].unsqueeze(1).broadcast_to([64, 15, 64])
                    for h in range(8):
                        tsb = tsrc[h % 2]
                        srcv = tbsrc_d[:, h * 960:(h + 1) * 960].rearrange("p (a b) -> p a b", a=15)
                        S.dma("sp", tsb[0:64], srcv, w=[("tsrc", h % 2, 0)])
                        S.dma("sp", tsb[64:128], srcv, w=[("tsrc", h % 2, 1)])
                        act(tsb[:], tsb[:], AF.Exp, r=[("tsrc", h % 2, 0), ("tsrc", h % 2, 1)], w=[("tsrc", h % 2, 0), ("tsrc", h % 2, 1)])
                        tt(TB2[0:64, h, 1:16, :], tsb[0:64], cmb_lo, ALU.mult, r=[("tsrc", h % 2, 0), "bvec", "TB2z"], w=[("TB2", h, 0)])
                        tt(TB2[64:128, h, 2:17, :], tsb[64:128], cmb_hi, ALU.mult, r=[("tsrc", h % 2, 1), "bvec", "TB2z"], w=[("TB2", h, 1)])
                    nat_r_k = ["KNw_own", ("KNw_h", 0), ("KNw_h", 1)]
                    QZn = sb(p2, "QZn", [128, 8, 1024], BF16)
                    memset(QZn[:], 0.0, w=["QZn0"])
                    for h in range(8):
                        pb0 = 64 * (h % 2)
                        if h % 2 == 0:
                            ts(QZn[pb0:pb0 + 64, h, :], QTl[pb0:pb0 + 64, h // 2, :], 1.0, ALU.mult,
                               r=[("QTl", h // 2, 0), ("QTl", h // 2, 1), "QZn0"], w=[("QZn", h)])
                        else:
                            act(QZn[pb0:pb0 + 64, h, :], QTl[pb0:pb0 + 64, h // 2, :], AF.Copy,
                                r=[("QTl", h // 2, 0), ("QTl", h // 2, 1), "QZn0"], w=[("QZn", h)])
                    units = []
                    ui = 0
                    for h in range(8):
                        for qt in range(2):
                            started = [False] * 4
                            for kb in range(12):
                                qbs = [qb for qb in range(4 * qt, 4 * qt + 4) if -2 <= kb - qb <= 6]
                                if not qbs:
                                    continue
                                firsts = [not started[qb % 4] for qb in qbs]
                                for qb in qbs:
                                    started[qb % 4] = True

                                def A(h=h, kb=kb, qbs=qbs, ui=ui):
                                    pb0 = 64 * (h % 2)
                                    qlo, qhi = qbs[0], qbs[-1]
                                    n = 2 * (qhi - qlo + 1) * 64
                                    tq0 = qlo * 128
                                    bank = ui % 2
                                    mm(PS[bank][:, 0:n], KNw[:, h // 2, kb * 128:(kb + 1) * 128],
                                       QZn[:, h, tq0:tq0 + n], start=True, stop=False,
                                       r=nat_r_k + [("QZn", h), "QZn0"], w=[pk(bank)], inc=False)
                                    mm(PS[bank][:, 0:n], amat[0:24, kb * 128:(kb + 1) * 128], bmat[0:24, tq0:tq0 + n],
                                       start=False, stop=True, r=["amat", "bmat"], w=[pk(bank)], inc=True)

                                def B(h=h, kb=kb, qbs=qbs, ui=ui):
                                    qlo, qhi = qbs[0], qbs[-1]
                                    nrows = 2 * (qhi - qlo + 1)
                                    n = nrows * 64
                                    bank = ui % 2
                                    e_, p_ = ETl[ui % 3], PTl[ui % 3]
                                    act(e_[:, 0:n], PS[bank][:, 0:n], AF.Exp, r=[pk(bank)], w=[("ETl", ui % 3)], scale=0.125)
                                    s0 = 12 - 2 * kb + 2 * qlo
                                    tt(p_[:, 0:n], e_[:, 0:n], TB2[:, h, s0:s0 + nrows, :].rearrange("p a b -> p (a b)"), ALU.mult,
                                       r=[("ETl", ui % 3), ("TB2", h, 0), ("TB2", h, 1), "TB2z"], w=[("PTl", ui % 3)])

                                def C(h=h, kb=kb, qbs=qbs, firsts=firsts, ui=ui):
                                    qlo = qbs[0]
                                    p_ = PTl[ui % 3]
                                    vkeys = ["VNw1"]
                                    if kb < 2:
                                        vkeys.append(("VNw_h", 0, kb))
                                    elif kb >= 10:
                                        vkeys.append(("VNw_h", 1, kb - 10))
                                    else:
                                        vkeys.append(("VNw_own", (kb - 2) // 2, (kb - 2) % 2))
                                    for qb, fst in zip(qbs, firsts):
                                        qi = qb % 4
                                        mm(PS[4 + qi][:, 0:65], p_[:, (qb - qlo) * 128:(qb - qlo + 1) * 128], VNw[:, kb, h, :],
                                           start=fst, stop=False, r=[("PTl", ui % 3)] + vkeys, w=[pk(4 + qi)], inc=False)
                                units.append((A, B, C))
                                ui += 1
                            for kb in range(4):
                                def A(h=h, kb=kb, qt=qt, ui=ui):
                                    pb0 = 64 * (h % 2)
                                    bank = ui % 2
                                    mm(PS[bank][:, :], KNc[:, h // 2, kb * 128:(kb + 1) * 128],
                                       QZn[:, h, qt * 512:(qt + 1) * 512], start=True, stop=True,
                                       r=["KNc", ("QZn", h), "QZn0"], w=[pk(bank)], inc=True)

                                def B(ui=ui):
                                    bank = ui % 2
                                    act(ETl[ui % 3][:], PS[bank][:, :], AF.Exp, r=[pk(bank)], w=[("ETl", ui % 3)], scale=0.125)

                                def C(h=h, kb=kb, qt=qt, ui=ui):
                                    e_ = ETl[ui % 3]
                                    for qi in range(4):
                                        mm(PS[4 + qi][:, 0:65], e_[:, qi * 128:(qi + 1) * 128], VNc[:, kb, h, :],
                                           start=False, stop=(kb == 3), r=[("ETl", ui % 3), ("VNc", kb), "VNc1"], w=[pk(4 + qi)],
                                           inc=(kb == 3))
                                    if kb == 3:
                                        for qi in range(4):
                                            qb = 4 * qt + qi
                                            attn_norm_nat(PS[4 + qi], otok[:, qb, h * 64:(h + 1) * 64], ("otok", qb), [pk(4 + qi)])
                                units.append((A, B, C))
                                ui += 1
                    pipeline(units)
                    S.barrier()
                if CUT < 4:
                    raise _Cut()
                with contextlib.ExitStack() as p3:
                    KDh = [sb(p3, f"KDh{i}", [128, 4096], BF16) for i in range(2)]
                    VDh = [sb(p3, f"VDh{i}", [128, 32, 129], BF16) for i in range(2)]
                    KDc = sb(p3, "KDc", [128, 4, 512], BF16)
                    VDc = sb(p3, "VDc", [128, 4, 4, 129], BF16)
                    for i in range(2):
                        memset(VDh[i][:, :, 128:129], 1.0, w=[("VDh1", i)])
                    memset(VDc[:, :, :, 128:129], 1.0, w=["VDc1"])
                    S.dma("pool", KDc[:], cdkT_d.rearrange("(c p) k -> p c k", p=128), w=["KDc"])
                    for kb in range(4):
                        S.dma("pool", VDc[:, kb, :, 0:128], cdv_d[kb * 128:(kb + 1) * 128, :].rearrange("p (h d) -> p h d", h=4),
                              w=[("VDc", kb)])
                    QZd = sb(p3, "QZd", [128, 4, 2, 1024], BF16)
                    memset(QZd[:], 0.0, w=["QZd0"])
                    for h in range(4):
                        ts(QZd[0:64, h, 0, :], QTl[0:64, 4 + h, :], 1.0, ALU.mult,
                           r=[("QTl", 4 + h, 0), ("QTl", 4 + h, 1), "QZd0"], w=[("QZd", h, 0)])
                        act(QZd[64:128, h, 1, :], QTl[64:128, 4 + h, :], AF.Copy,
                            r=[("QTl", 4 + h, 0), ("QTl", 4 + h, 1), "QZd0"], w=[("QZd", h, 1)])

                    def load_head(h):
                        hb_ = h % 2
                        S.dma("sp", KDh[hb_][:].rearrange("p (r t) -> p r t", r=4), gviews[1][:, h].rearrange("r p t -> p r t"),
                              w=[("KDh", hb_)])
                        for rr in range(4):
                            for s4 in range(4):
                                r0_ = rr * 512 + s4 * 128
                                srcv = gout_b[3][r0_:r0_ + 128, :].rearrange("p (hf x) -> p hf x", hf=2)
                                S.dma("sp", VDh[hb_][:, rr * 8 + 2 * s4:rr * 8 + 2 * s4 + 2, 0:128],
                                      srcv[:, :, h * 128:(h + 1) * 128], w=[("VDh", hb_, rr * 4 + s4)])

                    load_head(0)
                    load_head(1)
                    units = []
                    ui = 0
                    for h in range(4):
                        hb_ = h % 2
                        for qt in range(2):
                            for j in range(2):
                                for kb in range(36):
                                    def A(h=h, hb_=hb_, qt=qt, j=j, kb=kb, ui=ui):
                                        bank = ui % 2
                                        if kb < 32:
                                            lhs = KDh[hb_][:, kb * 128:(kb + 1) * 128]
                                            lk_ = [("KDh", hb_)]
                                        else:
                                            lhs = KDc[:, h, (kb - 32) * 128:(kb - 31) * 128]
                                            lk_ = ["KDc"]
                                        mm(PS[bank][:, :], lhs, QZd[:, h, j, qt * 512:(qt + 1) * 512], start=True, stop=True,
                                           r=lk_ + [("QZd", h, j), "QZd0"], w=[pk(bank)], inc=True)

                                    def B(ui=ui):
                                        bank = ui % 2
                                        act(ETl[ui % 3][:], PS[bank][:, :], AF.Exp, r=[pk(bank)], w=[("ETl", ui % 3)], scale=0.125)

                                    def C(h=h, hb_=hb_, qt=qt, j=j, kb=kb, ui=ui):
                                        e_ = ETl[ui % 3]
                                        if kb < 32:
                                            vv = VDh[hb_][:, kb, :]
                                            vk_ = [("VDh", hb_, kb // 2), ("VDh1", hb_)]
                                        else:
                                            vv = VDc[:, kb - 32, h, :]
                                            vk_ = [("VDc", kb - 32), "VDc1"]
                                        for qi in range(4):
                                            mm(PS[4 + qi][:, 0:129], e_[:, qi * 128:(qi + 1) * 128], vv,
                                               start=(kb == 0), stop=(kb == 35), r=[("ETl", ui % 3)] + vk_, w=[pk(4 + qi)], inc=(kb == 35))
                                        if kb == 35:
                                            if j == 0:
                                                for qi in range(4):
                                                    act(o1s[:, qi, :], PS[4 + qi][:, 0:129], AF.Copy, r=[pk(4 + qi)], w=[("o1s", qi)])
                                            else:
                                                for qi in range(4):
                                                    qb = 4 * qt + qi
                                                    attn_norm_diff(o1s[:, qi, :], PS[4 + qi][:, 0:129],
                                                                   otok[:, qb, 512 + h * 128:512 + (h + 1) * 128], ("otok", qb),
                                                                   [("o1s", qi), pk(4 + qi)])
                                                if qt == 1 and h + 2 < 4:
                                                    load_head(h + 2)
                                    units.append((A, B, C))
                                    ui += 1
                    pipeline(units)
                    S.barrier()
                if CUT < 5:
                    raise _Cut()
                with contextlib.ExitStack() as p4:
                    out_project(p4, otok, 8, 1024, 1)
                    S.barrier()
            S.barrier()

    def final_norm():
        with contextlib.ExitStack() as ph:
            bv = sb(ph, "fnb", [128, 1024])
            S.dma("sp", bv[:], bvec_d[:, BV_FN:BV_FN + 1024], w=["fnb"])
            tk = [sb(ph, f"ftk{i}", [128, 1024]) for i in range(2)]
            ob = [sb(ph, f"fob{i}", [128, 1024]) for i in range(2)]
            fsq = sb(ph, "fsq", [128, 1024])
            sm = sb(ph, "fsm", [128, 2, 4])
            for tb in range(16):
                par = tb % 2
                ti = tidx(tb * 128)
                for c in range(8):
                    bank = par * 2 + c // 4
                    S.op("pe", lambda e, bank=bank, c=c, tb=tb: e.transpose(
                        out=PS[bank][:, (c % 4) * 128:(c % 4 + 1) * 128], in_=X[:, c, tb * 128:(tb + 1) * 128], identity=ident[:]),
                        r=[xk(c, ti), "ident"], w=[pk(bank)])
                for hf in range(2):
                    act(tk[par][:, hf * 512:(hf + 1) * 512], PS[par * 2 + hf][:, :], AF.Copy, r=[pk(par * 2 + hf)], w=[("ftk", par, hf)])
                act(fsq[:], tk[par][:], AF.Square, r=[("ftk", par, 0), ("ftk", par, 1)], w=["fsq", ("fsm", par, 0)],
                    accum_out=sm[:, par, 0:1])
                act(sm[:, par, 1:2], sm[:, par, 0:1], AF.Sqrt, r=[("fsm", par, 0)], w=[("fsm", par, 1)], bias=epsc[:, 0:1], scale=1.0 / D)
                recip(sm[:, par, 2:3], sm[:, par, 1:2], r=[("fsm", par, 1)], w=[("fsm", par, 2)])
                stt(ob[par][:], tk[par][:], sm[:, par, 2:3], bv[:], ALU.mult, ALU.mult,
                    r=[("ftk", par, 0), ("ftk", par, 1), ("fsm", par, 2), "fnb"], w=[("fob", par)])
                tk_ = S.dma("sp", y_d[tb * 128:(tb + 1) * 128, :], ob[par][:], r=[("fob", par)])
                S.out_toks.append(tk_)
            S.barrier()

    ffn(0, 0, TILES_ALL + [TILE_HALO])
    even_mixer()
    ffn(0, 1, TILES_ALL)
    if stage == "full":
        ffn(1, 0, TILES_ALL)
        try:
            odd_mixer()
            ffn(1, 1, TILES_ALL)
        except _Cut:
            S.barrier()
            cut_hit.append(1)
    final_norm()
    for t in S.out_toks:
        S.wait("sp", t)
    S.barrier()
    if not cut_hit:
        es.close()
    return nc


def _cnt_inv(w, t, n):
    hw = w // 2
    lo = np.clip(t - hw, 0, n - 1)
    hi = np.clip(t + hw - 1, 0, n - 1)
    return (1.0 / (hi - lo + 1)).astype(np.float32)


def _partner():
    p = np.arange(128)
    within = p % 64
    part = (within % 32) // 16
    return np.where(part == 0, p + 16, p - 16), part, within


def prepare_inputs(inp):
    f = lambda a: np.ascontiguousarray(np.asarray(a, dtype=np.float32))
    g = {k: np.asarray(v) for k, v in inp.items()}
    p = np.arange(128)
    shared = {}
    modw_all = g["mod_w"].reshape(2, 8, 128, 8, 1152).transpose(3, 0, 2, 1, 4)
    w1in = np.empty((2, 2, NJ, 128, 2048), np.float32)
    w1out = np.empty((2, 2, NJ, 128, 1024), np.float32)
    for l in range(2):
        for wi, nm in enumerate(("ffn1", "ffn2")):
            W = g[nm + "_w_in"][l]
            a = W[:, :DFF].reshape(8, 128, NJ, 128).transpose(2, 1, 0, 3)
            b = W[:, DFF:].reshape(8, 128, NJ, 128).transpose(2, 1, 0, 3)
            w1in[l, wi] = np.concatenate([a, b], axis=3).reshape(NJ, 128, 2048)
            w1out[l, wi] = g[nm + "_w_out"][l].reshape(NJ, 128, 1024)
    shared["w1in"] = w1in
    shared["w1out"] = w1out
    shared["ewin"] = f(g["even_w_in"][0].reshape(8, 128, 2048))
    shared["poolw"] = f(g["pool_w"][0])
    shared["mixw"] = f(g["mix_w_out"].reshape(2, 8, 128, 1024))
    partner, part, within = _partner()
    W = g["odd_w_in"][0]
    permidx = (np.arange(4)[:, None] * 128 + partner[None, :]).reshape(-1)
    Wext = np.concatenate([W, W[:, 1536 + permidx], W[:, 2048 + permidx]], axis=1)
    shared["owin"] = f(Wext.reshape(8, 128, 4096))
    shared["ident"] = np.eye(128, dtype=np.float32)
    W4 = [2, 4, 8, 16]
    shared["icc"] = f(np.broadcast_to(np.concatenate([_cnt_inv(w, np.arange(256), 256) for w in W4])[None, :], (128, 1024)))
    rpb = g["nat_rpb"][0]
    kc = np.arange(64)[:, None]
    qc = np.arange(64)[None, :]
    dc = np.clip(kc - qc + 15, 0, 30)
    tb = rpb[:, ::-1, :][:, :, dc]
    shared["tbsrc"] = f(tb.transpose(2, 0, 1, 3).reshape(64, 8 * 15 * 64))
    shared["amat"] = f((np.arange(1536)[None, :] // 64 == np.arange(24)[:, None]).astype(np.float32))
    c0 = np.clip(np.arange(64) - 8, 0, 48)
    colmask = ((kc >= c0[None, :]) & (kc <= c0[None, :] + 15)).astype(np.float32)
    per_core = []
    quarter = 16
    inv = 1.0 / (10000.0 ** (np.arange(quarter) / quarter))
    for c in range(8):
        b, r = c // 4, c % 4
        d = dict(shared)
        xin = np.zeros((NTOK, D), np.float32)
        xin[0:1024] = g["x_prompt"][4 * c:4 * c + 4].reshape(1024, D)
        xin[1024:2048] = g["x_sample"][b, 1024 * r:1024 * (r + 1)]
        if r > 0:
            xin[2048:2056] = g["x_sample"][b, 1024 * r - 8:1024 * r]
        if r < 3:
            xin[2056:2064] = g["x_sample"][b, 1024 * (r + 1):1024 * (r + 1) + 8]
        d["xin"] = xin
        pv = np.zeros((128, PV_N), np.float32)
        d["modw"] = f(np.stack([modw_all[r + 4 * sl, l] for sl in range(2) for l in range(2)]).reshape(4, 128, 8 * 1152))
        conds = [g["c_ctx"], g["c"][0], g["c"][1]]
        for j in range(3):
            pv[:, PV_COND + np.arange(8) * 3 + j] = conds[j].reshape(8, 128).T
        pv[:, PV_SELB + b] = 1.0
        for sl in range(2):
            for l in range(2):
                q4 = sl * 2 + l
                slab = r + 4 * sl
                pv[:, PV_MODB + q4 * 9:PV_MODB + (q4 + 1) * 9] = g["mod_b"][l].reshape(72, 128)[slab * 9:(slab + 1) * 9].T
        for l in range(2):
            for n, nm in enumerate(("norm_ffn1", "norm_mix", "norm_ffn2")):
                o = PV_GAIN + (l * 3 + n) * 8
                pv[:, o:o + 8] = g[nm][l].reshape(8, 128).T
        pv[:, PV_PSC:PV_PSC + 4] = g["pool_scale"][0].reshape(4, 128).T
        for j in range(3):
            pv[:, PV_CONV + j * 4:PV_CONV + j * 4 + 4] = g["conv_w"][0, j].reshape(4, 128).T
        pv[:, PV_HMASK:PV_HMASK + 8] = 1.0 if r > 0 else 0.0
        pv[:, PV_HMASK + 8:PV_HMASK + 16] = 1.0 if r < 3 else 0.0
        if r > 0:
            pv[:, PV_SELP + r - 1] = 1.0
        if r < 3:
            pv[:, PV_SELN + r + 1] = 1.0
        d["pvec"] = pv
        bv = np.zeros((128, BV_N), np.float32)
        bv[:, BV_FN:BV_FN + 1024] = g["final_norm"][None, :]
        bv[:, BV_DN:BV_DN + 128] = g["diff_norm"][0][None, :]
        bv[:, BV_LAM:BV_LAM + 256] = g["diff_lambda"][0].reshape(-1)[None, :]
        bv[:, BV_CM:BV_CM + 64] = colmask[p % 64, :]
        d["bvec"] = bv
        tabs = 1024 * r + np.arange(1024)
        d["icl"] = f(np.broadcast_to(np.concatenate([_cnt_inv(w, tabs, 4096) for w in W4])[None, :], (128, 4096)))
        row = (tabs // 64).astype(np.float64)
        col = (tabs % 64).astype(np.float64)
        a_ = within // 32
        i_ = within % 16
        pos = np.where(a_[:, None] == 0, row[None, :], col[None, :])
        ang = pos * inv[i_][:, None]
        d["ropec"] = np.cos(ang).astype(np.float32)
        d["ropes"] = (np.sin(ang).astype(np.float32) * np.where(part == 0, -1.0, 1.0)[:, None]).astype(np.float32)
        R0 = 16 * r
        lk = np.arange(24)[:, None]
        lq = (np.arange(1024) // 64)[None, :]
        kr = R0 - 4 + lk
        qr = R0 + lq
        r0 = np.clip(qr - 4, 0, 56)
        valid = (kr >= 0) & (kr <= 63) & (kr >= r0) & (kr <= r0 + 7)
        d["bmat"] = np.where(valid, 0.0, NEG).astype(np.float32)
        d["cnkT"] = f(g["cache_nat_k"][b, 0].reshape(512, 512).T)
        d["cnv"] = f(g["cache_nat_v"][b, 0].reshape(512, 512))
        d["cdkT"] = f(g["cache_diff_k"][b, 0].reshape(512, 512).T)
        d["cdv"] = f(g["cache_diff_v"][b, 0].reshape(512, 512))
        per_core.append(d)
    return per_core


def assemble(results):
    y_prompt = np.empty((32, 256, D), np.float32)
    y_sample = np.empty((2, 4096, D), np.float32)
    nnk = np.empty((32, 1, 256, 8, 64), np.float32)
    nnv = np.empty((32, 1, 256, 8, 64), np.float32)
    ndk = np.empty((32, 1, 256, 4, 2, 64), np.float32)
    ndv = np.empty((32, 1, 256, 4, 128), np.float32)
    for c in range(8):
        b, r = c // 4, c % 4
        res = results[c]
        y = np.asarray(res["y"])
        y_prompt[4 * c:4 * c + 4] = y[0:1024].reshape(4, 256, D)
        y_sample[b, 1024 * r:1024 * (r + 1)] = y[1024:2048]
        nnk[4 * c:4 * c + 4, 0] = np.asarray(res["onk"]).reshape(4, 256, 8, 64)
        nnv[4 * c:4 * c + 4, 0] = np.asarray(res["onv"]).reshape(4, 256, 8, 64)
        ndk[4 * c:4 * c + 4, 0] = np.asarray(res["odk"]).reshape(4, 256, 4, 2, 64)
        ndv[4 * c:4 * c + 4, 0] = np.asarray(res["odv"]).reshape(4, 256, 4, 128)
    return (y_prompt, y_sample, nnk, nnv, ndk, ndv)


def kernel(stage="full", **inputs):
    per_core = prepare_inputs(inputs)
    nc = build_program(stage)
    res = run_bass_kernel_spmd(nc, per_core, core_ids=list(range(8)))
    return assemble(res.results)
```

```python
import contextlib
import math
import numpy as np
import ml_dtypes
import concourse.bass as bass
import concourse.mybir as mybir
from concourse.bass_utils import run_bass_kernel_spmd

F32 = mybir.dt.float32
BF16 = mybir.dt.bfloat16
AF = mybir.ActivationFunctionType
ALU = mybir.AluOpType

D = 1024
DFF = 2816
NJ = 22
EPS = 1e-6
NTOK = 2064
GROUPS = [list(range(0, 6)), list(range(6, 12)), list(range(12, 17)), list(range(17, 22))]
TILES_ALL = [(0, 512, 0), (512, 512, 0), (1024, 512, 1), (1536, 512, 1)]
TILE_HALO = (2048, 16, 1)
NEG = -30000.0
CUT = 99


class _Cut(Exception):
    pass

PV_COND = 0
PV_MODB = 24
PV_GAIN = 60
PV_PSC = 108
PV_CONV = 112
PV_HMASK = 124
PV_SELP = 140
PV_SELN = 144
PV_SELB = 148
PV_N = 152
BV_FN = 0
BV_DN = 1024
BV_LAM = 1152
BV_CM = 1408
BV_N = 1472


class Sched:
    def __init__(self, nc, es):
        self.nc = nc
        self.eng = dict(pe=nc.tensor, act=nc.scalar, dve=nc.vector, pool=nc.gpsimd, sp=nc.sync)
        self.sem = {k: es.enter_context(nc.semaphore("s_" + k)) for k in ("pe", "act", "dve", "pool")}
        self.seq = dict(pe=0, act=0, dve=0, pool=0)
        self.known = {k: {} for k in self.eng}
        self.st = {}
        self.dsem = {}
        self.dval = {}
        self.dnext = {}
        for q, n in (("sp", 20), ("pool", 20)):
            self.dsem[q] = [es.enter_context(nc.semaphore(f"d_{q}{i}")) for i in range(n)]
            self.dval[q] = [0] * n
            self.dnext[q] = 0
        self.ccsem = es.enter_context(nc.semaphore("ccsem"))
        self.ccsem2 = es.enter_context(nc.semaphore("ccsem2"))
        self.out_toks = []

    def wait(self, eng, tok):
        sem, val, src = tok
        if src == "pe" and eng == "pe":
            return
        kn = self.known[eng]
        key = id(sem)
        if kn.get(key, 0) >= val:
            return
        self.eng[eng].wait_ge(sem, val)
        kn[key] = val

    def _deps(self, r, w):
        deps = []
        for k in r:
            s = self.st.get(k)
            if s:
                deps += list(s[0].values())
        for k in w:
            s = self.st.get(k)
            if s:
                deps += list(s[0].values())
                deps += list(s[1].values())
        return deps

    def _record(self, tok, r, w):
        sem, val, src = tok
        kk = src if src else (id(sem), val)
        for k in r:
            self.st.setdefault(k, [{}, {}])[1][kk] = tok
        for k in w:
            self.st[k] = [{kk: tok}, {}]

    def op(self, eng, fn, r=(), w=(), inc=True):
        for t in self._deps(r, w):
            self.wait(eng, t)
        ins = fn(self.eng[eng])
        if inc:
            self.seq[eng] += 1
            ins.then_inc(self.sem[eng], 1)
            tok = (self.sem[eng], self.seq[eng], eng)
        else:
            tok = (self.sem[eng], self.seq[eng] + 1, eng)
        self._record(tok, r, w)
        return tok

    def dma(self, q, out, in_, r=(), w=(), **kw):
        for t in self._deps(r, w):
            self.wait(q, t)
        i = self.dnext[q]
        self.dnext[q] = (i + 1) % len(self.dsem[q])
        sem = self.dsem[q][i]
        if self.dval[q][i] > 0:
            self.wait(q, (sem, self.dval[q][i], None))
        ins = self.eng[q].dma_start(out=out, in_=in_, **kw)
        self.dval[q][i] += 16
        ins.then_inc(sem, 16)
        tok = (sem, self.dval[q][i], None)
        self._record(tok, r, w)
        return tok

    def all_toks(self):
        toks = [(self.sem[e], self.seq[e], e) for e in self.seq if self.seq[e] > 0]
        for q in self.dsem:
            for i, s in enumerate(self.dsem[q]):
                if self.dval[q][i] > 0:
                    toks.append((s, self.dval[q][i], None))
        return toks

    def barrier(self):
        toks = self.all_toks()
        for e in self.eng:
            for t in toks:
                self.wait(e, t)
        self.st = {}


def build_program(stage="full", debug=False):
    nc = bass.Bass("TRN2", target_bir_lowering=False)
    es = contextlib.ExitStack()

    def din(name, shape, dt=F32):
        return nc.dram_tensor(name, list(shape), dt, kind="ExternalInput").ap()

    def dout(name, shape, dt=F32):
        return nc.dram_tensor(name, list(shape), dt, kind="ExternalOutput").ap()

    xin = din("xin", [NTOK, D])
    pvec_d = din("pvec", [128, PV_N])
    bvec_d = din("bvec", [128, BV_N])
    modw_d = din("modw", [4, 128, 8 * 1152])
    w1in_d = din("w1in", [2, 2, NJ, 128, 2048])
    w1out_d = din("w1out", [2, 2, NJ, 128, 1024])
    ewin_d = din("ewin", [8, 128, 2048])
    poolw_d = din("poolw", [4, 128, 128])
    mixw_d = din("mixw", [2, 8, 128, 1024])
    owin_d = din("owin", [8, 128, 4096])
    ident_d = din("ident", [128, 128])
    icc_d = din("icc", [128, 4 * 256])
    icl_d = din("icl", [128, 4 * 1024])
    ropec_d = din("ropec", [128, 1024])
    ropes_d = din("ropes", [128, 1024])
    tbsrc_d = din("tbsrc", [64, 8 * 15 * 64])
    amat_d = din("amat", [24, 1536])
    bmat_d = din("bmat", [24, 1024])
    cnkT_d = din("cnkT", [512, 512])
    cnv_d = din("cnv", [512, 512])
    cdkT_d = din("cdkT", [512, 512])
    cdv_d = din("cdv", [512, 512])

    y_d = dout("y", [2048, D])
    onk_d = dout("onk", [1024, 512])
    onv_d = dout("onv", [1024, 512])
    odk_d = dout("odk", [1024, 512])
    odv_d = dout("odv", [1024, 512])
    dbg_d = dout("dbg", [8, 128, 2048]) if debug else None

    mg_in = nc.dram_tensor("mg_in", [128, 108], F32, kind="Internal").ap()
    mg_out = nc.dram_tensor("mg_out", [512, 108], F32, kind="Internal", addr_space="Local").ap()
    gin_f = [nc.dram_tensor(f"gin{i}", [512, 512], F32, kind="Internal").ap() for i in range(4)]
    gout_f = [nc.dram_tensor(f"gout{i}", [2048, 512], F32, kind="Internal", addr_space="Local").ap() for i in range(4)]
    gin_b = [g_.bitcast(BF16) for g_ in gin_f]
    gout_b = [g_.bitcast(BF16) for g_ in gout_f]

    S = Sched(nc, es)
    cut_hit = []

    _uid = [0]

    def sb(stack, name, shape, dt=F32):
        _uid[0] += 1
        return stack.enter_context(nc.sbuf_tensor(f"{name}_{_uid[0]}", list(shape), dt))

    X = sb(es, "X", [128, 8, NTOK])
    pvec = sb(es, "pvec_s", [128, PV_N])
    mods = sb(es, "mods", [128, 2, 72, 2])
    gsv = sb(es, "gsv", [128, 2, 3, 8, 2])
    ghv = sb(es, "ghv", [128, 2, 2, 8, 2])
    ident = sb(es, "ident_s", [128, 128])
    identb = sb(es, "identb", [128, 128], BF16)
    onesb = sb(es, "onesb", [128, 128], BF16)
    PS = [es.enter_context(nc.psum_tensor(f"ps{i}", [128, 512], F32)) for i in range(8)]

    def pk(i):
        return ("ps", i)

    def mm(out, lhsT, rhs, start, stop, r, w, inc):
        return S.op("pe", lambda e: e.matmul(out, lhsT=lhsT, rhs=rhs, start=start, stop=stop), r=r, w=w, inc=inc)

    def act(out, in_, func, r, w, bias=None, scale=None, accum_out=None):
        kw = {}
        if bias is not None:
            kw["bias"] = bias
        if scale is not None:
            kw["scale"] = scale
        if accum_out is not None:
            kw["accum_out"] = accum_out
        return S.op("act", lambda e: e.activation(out=out, in_=in_, func=func, **kw), r=r, w=w)

    def tt(out, in0, in1, op, r, w, eng="dve"):
        return S.op(eng, lambda e: e.tensor_tensor(out=out, in0=in0, in1=in1, op=op), r=r, w=w)

    def ts(out, in0, s1, op0, r, w, s2=None, op1=None, eng="dve"):
        if op1 is None:
            return S.op(eng, lambda e: e.tensor_scalar(out=out, in0=in0, scalar1=s1, scalar2=None, op0=op0), r=r, w=w)
        return S.op(eng, lambda e: e.tensor_scalar(out=out, in0=in0, scalar1=s1, scalar2=s2, op0=op0, op1=op1), r=r, w=w)

    def stt(out, in0, scalar, in1, op0, op1, r, w):
        return S.op("dve", lambda e: e.scalar_tensor_tensor(out=out, in0=in0, scalar=scalar, in1=in1, op0=op0, op1=op1), r=r, w=w)

    def memset(ap, val, w, eng="dve"):
        return S.op(eng, lambda e: e.memset(ap, val), w=w)

    def recip(out, in_, r, w):
        return S.op("dve", lambda e: e.reciprocal(out=out, in_=in_), r=r, w=w)

    def xk(c, ti):
        return ("X", c, ti)

    def tidx(t0):
        return t0 // 512

    S.dma("sp", pvec[:], pvec_d, w=["pvec"])
    S.dma("sp", ident[:], ident_d, w=["ident"])
    S.op("dve", lambda e: e.tensor_copy(out=identb[:], in_=ident[:]), r=["ident"], w=["identb"])
    memset(onesb[:], 1.0, w=["onesb"])

    with contextlib.ExitStack() as ph:
        xt = [sb(ph, f"xt{i}", [128, D]) for i in range(2)]
        scond = sb(ph, "scond", [128, 8, 3])
        mwb = [sb(ph, f"mwb{i}", [128, 8 * 1152]) for i in range(2)]
        mres = sb(ph, "mres", [128, 4, 9, 3])
        mall = sb(ph, "mall", [128, 4, 108])
        act(scond[:], pvec[:, PV_COND:PV_COND + 24].rearrange("p (k j) -> p k j", j=3), AF.Silu, r=["pvec"], w=["scond"])
        for q4 in range(4):
            S.dma("sp", mwb[q4 % 2][:], modw_d[q4], w=[("mwb", q4 % 2)])
            wv = mwb[q4 % 2][:].rearrange("p (k n) -> p k n", k=8)
            for cc in range(9):
                col = (q4 * 9 + cc) * 3
                for k in range(8):
                    mm(PS[2][:, col:col + 3], wv[:, k, cc * 128:(cc + 1) * 128], scond[:, k, :],
                       start=(k == 0), stop=(k == 7), r=[("mwb", q4 % 2), "scond"], w=[pk(2)], inc=(k == 7))
        for q4 in range(4):
            for j in range(3):
                tt(mres[:, q4, :, j], PS[2][:, q4 * 27:(q4 + 1) * 27].rearrange("p (c j) -> p c j", j=3)[:, :, j],
                   pvec[:, PV_MODB + q4 * 9: PV_MODB + (q4 + 1) * 9], ALU.add, r=[pk(2), "pvec"], w=[("mres", q4, j)])
        tk_ = S.dma("pool", mg_in, mres[:].rearrange("p q c j -> p (q c j)"), r=[("mres", q4, j) for q4 in range(4) for j in range(3)])
        S.wait("pool", tk_)
        nc.gpsimd.collective_compute("AllGather", ALU.bypass, replica_groups=[[0, 1, 2, 3], [4, 5, 6, 7]],
                                     ins=[mg_in], outs=[mg_out]).then_inc(S.ccsem2)
        nblk = 16
        it = 0
        for tb in range(nblk + 1):
            rows = 128 if tb < nblk else 16
            buf = xt[tb % 2]
            S.dma("sp", buf[0:rows, :], xin[tb * 128: tb * 128 + rows, :], w=[("xt", tb % 2)])
            for c in range(8):
                bank = it % 2
                it += 1
                S.op("pe", lambda e, c=c, bank=bank: e.transpose(out=PS[bank][:, 0:rows], in_=buf[0:rows, c * 128:(c + 1) * 128],
                                                           identity=ident[0:rows, 0:rows]),
                     r=[("xt", tb % 2), "ident"], w=[pk(bank)])
                dst = X[:, c, tb * 128: tb * 128 + rows]
                ti = tidx(tb * 128)
                if c % 2 == 0:
                    act(dst, PS[bank][:, 0:rows], AF.Copy, r=[pk(bank)], w=[("Xld", c, tb)])
                else:
                    ts(dst, PS[bank][:, 0:rows], 1.0, ALU.mult, r=[pk(bank)], w=[("Xld", c, tb)])
        S.wait("pool", (S.ccsem2, 1, None))
        S.dma("pool", mall[:], mg_out.rearrange("(r p) n -> p r n", p=128), w=["mall"])
        for l in range(2):
            for sl in range(2):
                q4 = sl * 2 + l
                mv = mall[:, :, q4 * 27:(q4 + 1) * 27].rearrange("p r (c j) -> p r c j", j=3)
                o0 = mods[:, l, sl * 36:(sl + 1) * 36, 0].rearrange("p (r c) -> p r c", r=4)
                o1 = mods[:, l, sl * 36:(sl + 1) * 36, 1].rearrange("p (r c) -> p r c", r=4)
                ts(o0, mv[:, :, :, 0], 1.0, ALU.mult, r=["mall"], w=[("mods", l, 0, sl)])
                ts(o1, mv[:, :, :, 1], pvec[:, PV_SELB:PV_SELB + 1], ALU.mult, r=["mall", "pvec"], w=[("mods", l, 1, sl)])
                stt(o1, mv[:, :, :, 2], pvec[:, PV_SELB + 1:PV_SELB + 2], o1, ALU.mult, ALU.add,
                    r=["mall", "pvec", ("mods", l, 1, sl)], w=[("mods", l, 1, sl)])
        for l in range(2):
            for n in range(3):
                for j in range(2):
                    stt(gsv[:, l, n, :, j], mods[:, l, (1 + 3 * n) * 8:(2 + 3 * n) * 8, j], 1.0,
                        pvec[:, PV_GAIN + (l * 3 + n) * 8: PV_GAIN + (l * 3 + n + 1) * 8], ALU.add, ALU.mult,
                        r=[("mods", l, j, 0), ("mods", l, j, 1), "pvec"], w=[("gsv", l, n, j)])
            for wi in range(2):
                for j in range(2):
                    i0 = 2 if wi == 0 else 8
                    ts(ghv[:, l, wi, :, j], mods[:, l, i0 * 8:(i0 + 1) * 8, j], 0.5, ALU.mult,
                       r=[("mods", l, j, 0), ("mods", l, j, 1)], w=[("ghv", l, wi, j)])
        S.barrier()

    def modcol(l, i, c, j):
        return mods[:, l, i * 8 + c, j:j + 1]

    def modnorm(stack_bufs, t0, tn, cj, l, n, ishift, out, okey):
        sq, rt, rstd, tmp = stack_bufs
        ti = tidx(t0)
        for c in range(8):
            act(sq[:, c, 0:tn], X[:, c, t0:t0 + tn], AF.Square, r=[xk(c, ti)], w=[("sq", c)])
        for c in range(8):
            mm(PS[6][:, 0:tn], onesb[:], sq[:, c, 0:tn], start=(c == 0), stop=(c == 7),
               r=[("sq", c), "onesb"], w=[pk(6)], inc=(c == 7))
        act(rt[:, 0:tn], PS[6][:, 0:tn], AF.Ln, r=[pk(6)], w=["rt"], bias=epsc[:, 0:1], scale=1.0 / D)
        act(rstd[:, 0:tn], rt[:, 0:tn], AF.Exp, r=["rt"], w=["rstd"], scale=-0.5)
        for c in range(8):
            tb_ = tmp[c % 2]
            stt(tb_[:, 0:tn], X[:, c, t0:t0 + tn], gsv[:, l, n, c, cj:cj + 1], rstd[:, 0:tn], ALU.mult, ALU.mult,
                r=[xk(c, ti), "rstd"], w=[("tmp", c % 2)])
            act(out(c), tb_[:, 0:tn], AF.Identity, r=[("tmp", c % 2)], w=[okey(c)], bias=modcol(l, ishift, c, cj))

    epsc = sb(es, "epsc", [128, 1])
    memset(epsc[:], EPS, w=["epsc"])
    S.barrier()

    def dbg_dump(idx):
        if debug:
            for c in range(8):
                S.dma("sp", dbg_d[idx, :, c * 256:(c + 1) * 256].rearrange("p (a b) -> p a b", a=1),
                      X[:, c:c + 1, 0:256], r=[xk(c, 0)])
            S.barrier()

    def ffn(l, wi, tiles):
        n = 0 if wi == 0 else 2
        ishift = 0 if wi == 0 else 6
        with contextlib.ExitStack() as ph:
            xn = sb(ph, "xn", [128, 8, NTOK], BF16)
            hb = sb(ph, "hb", [128, 6, NTOK], BF16)
            win = [sb(ph, f"win{i}", [128, 8, 256], BF16) for i in range(3)]
            wo = [sb(ph, f"wo{i}", [128, 6, 1024], BF16) for i in range(2)]
            sq = sb(ph, "sq", [128, 8, 512], BF16)
            rt = sb(ph, "rt", [128, 512])
            rstd = sb(ph, "rstd", [128, 512])
            tmp = [sb(ph, f"tmp{i}", [128, 512]) for i in range(2)]
            sl = [sb(ph, f"sl{i}", [128, 512], BF16) for i in range(2)]
            itc = 0
            for g, chunks in enumerate(GROUPS):
                for jj, j in enumerate(chunks):
                    slot = j % 3
                    S.dma("pool", win[slot][:].rearrange("p k n -> p (k n)"), w1in_d[l, wi, j], w=[("win", slot)])
                    for (t0, tn, cj) in tiles:
                        ti = tidx(t0)
                        if j == 0:
                            modnorm((sq, rt, rstd, tmp), t0, tn, cj, l, n, ishift,
                                    out=lambda c, t0=t0, tn=tn: xn[:, c, t0:t0 + tn], okey=lambda c, ti=ti: ("xn", c, ti))
                        par = itc % 2
                        itc += 1
                        pa, pb = PS[par], PS[2 + par]
                        for k in range(8):
                            mm(pa[:, 0:tn], win[slot][:, k, 0:128], xn[:, k, t0:t0 + tn], start=(k == 0), stop=(k == 7),
                               r=[("win", slot), ("xn", k, ti)], w=[pk(par)], inc=(k == 7))
                        for k in range(8):
                            mm(pb[:, 0:tn], win[slot][:, k, 128:256], xn[:, k, t0:t0 + tn], start=(k == 0), stop=(k == 7),
                               r=[("win", slot), ("xn", k, ti)], w=[pk(2 + par)], inc=(k == 7))
                        act(sl[par][:, 0:tn], pa[:, 0:tn], AF.Silu, r=[pk(par)], w=[("sl", par)])
                        tt(hb[:, jj, t0:t0 + tn], sl[par][:, 0:tn], pb[:, 0:tn], ALU.mult,
                           r=[("sl", par), pk(2 + par)], w=[("hb", jj, ti)])
                ws = g % 2
                nj = len(chunks)
                S.dma("pool", wo[ws][:, 0:nj, :], w1out_d[l, wi, chunks[0]:chunks[0] + nj].rearrange("j p n -> p j n"),
                      w=[("wo", ws)])
                ito = 0
                for (t0, tn, cj) in tiles:
                    ti = tidx(t0)
                    for m in range(8):
                        par = ito % 2
                        ito += 1
                        po = PS[4 + par]
                        for jj in range(nj):
                            mm(po[:, 0:tn], wo[ws][:, jj, m * 128:(m + 1) * 128], hb[:, jj, t0:t0 + tn],
                               start=(jj == 0), stop=(jj == nj - 1), r=[("wo", ws), ("hb", jj, ti)], w=[pk(4 + par)],
                               inc=(jj == nj - 1))
                        stt(X[:, m, t0:t0 + tn], po[:, 0:tn], ghv[:, l, wi, m, cj:cj + 1], X[:, m, t0:t0 + tn],
                            ALU.mult, ALU.add, r=[pk(4 + par), xk(m, ti)], w=[xk(m, ti)])
            S.barrier()

    def even_mixer():
        l = 0
        with contextlib.ExitStack() as ph:
            xn = sb(ph, "exn", [128, 8, 1040], BF16)
            yb = sb(ph, "eyb", [128, 8, 1024], BF16)
            ews = [sb(ph, f"ews{i}", [128, 8, 128], BF16) for i in range(3)]
            mixw = sb(ph, "mixw_s", [128, 8, 1024], BF16)
            poolw = sb(ph, "poolw_s", [128, 4, 128], BF16)
            Ul = [sb(ph, f"U{i}", [128, 1088]) for i in range(2)]
            U2l = [sb(ph, f"U2{i}", [128, 1088]) for i in range(2)]
            Pal = [sb(ph, "Pa0", [128, 1088])] * 2
            Pbl = [sb(ph, "Pb0", [128, 1088])] * 2
            GBl = [sb(ph, f"GB{i}", [128, 1024]) for i in range(2)]
            Acl = [sb(ph, "Ac0", [128, 1024])] * 2
            dbfl = [sb(ph, "dbf0", [128, 1024], BF16)] * 2
            cix = [0]
            icc = sb(ph, "icc_s", [128, 4, 256])
            icl = sb(ph, "icl_s", [128, 4, 1024])
            S.dma("sp", icc[:].rearrange("p g t -> p (g t)"), icc_d, w=["icc"])
            S.dma("sp", icl[:].rearrange("p g t -> p (g t)"), icl_d, w=["icl"])
            S.dma("pool", mixw[:], mixw_d[0].rearrange("k p n -> p k n"), w=["mixw"])
            S.dma("pool", poolw[:], poolw_d.rearrange("g p n -> p g n"), w=["poolw"])
            eit = [0]
            esl = [0]
            for half in range(2):
                if half == 0:
                    tiles = [(0, 512, 0), (512, 512, 0)]
                    nseq, L, base = 4, 256, 0
                else:
                    tiles = [(1024, 512, 1), (1536, 512, 1), TILE_HALO]
                    nseq, L, base = 1, 1024, 1024
                Lp = L + 16
                cj = half

                def xoff(t0):
                    return (t0 - base) if t0 < 2048 else (1024 + t0 - 2048)

                with contextlib.ExitStack() as pn:
                    sq = sb(pn, "sq", [128, 8, 512], BF16)
                    rt = sb(pn, "rt", [128, 512])
                    rstd = sb(pn, "rstd", [128, 512])
                    tmp = [sb(pn, f"tmp{i}", [128, 512]) for i in range(2)]
                    for (t0, tn, _) in tiles:
                        ti = tidx(t0)
                        o = xoff(t0)
                        modnorm((sq, rt, rstd, tmp), t0, tn, cj, l, 1, 3,
                                out=lambda c, o=o, tn=tn: xn[:, c, o:o + tn], okey=lambda c, ti=ti: ("exn", c, ti))
                    S.barrier()

                def pview(buf):
                    return buf[:, 0:nseq * Lp].rearrange("p (s t) -> p s t", s=nseq)

                def project(cidx, dst_buf, dkey, padded):
                    slot = esl[0] % 3
                    esl[0] += 1
                    S.dma("pool", ews[slot][:], ewin_d[:, :, cidx * 128:(cidx + 1) * 128].rearrange("k p n -> p k n"),
                          w=[("ews", slot)])
                    for (t0, tn, _) in tiles:
                        ti = tidx(t0)
                        o = xoff(t0)
                        bank = eit[0] % 2
                        eit[0] += 1
                        for k in range(8):
                            mm(PS[bank][:, 0:tn], ews[slot][:, k, :], xn[:, k, o:o + tn], start=(k == 0), stop=(k == 7),
                               r=[("ews", slot), ("exn", k, ti)], w=[pk(bank)], inc=(k == 7))
                        if t0 >= 2048:
                            if padded:
                                tt(dst_buf[:, 0:8], PS[bank][:, 0:8], pvec[:, PV_HMASK:PV_HMASK + 8], ALU.mult,
                                   r=[pk(bank), "pvec"], w=[dkey])
                                tt(dst_buf[:, 8 + L:16 + L], PS[bank][:, 8:16], pvec[:, PV_HMASK + 8:PV_HMASK + 16], ALU.mult,
                                   r=[pk(bank), "pvec"], w=[dkey])
                            continue
                        if padded:
                            if half == 0:
                                s0 = (t0 // 256)
                                dst = pview(dst_buf)[:, s0:s0 + 2, 8:8 + 256]
                                src = PS[bank][:, 0:512].rearrange("p (s t) -> p s t", s=2)
                            else:
                                dst = dst_buf[:, 8 + o:8 + o + tn]
                                src = PS[bank][:, 0:tn]
                        else:
                            dst = dst_buf[:, o:o + tn]
                            src = PS[bank][:, 0:tn]
                        act(dst, src, AF.Copy, r=[pk(bank)], w=[dkey])

                for pp in range(2):
                    memset(Ul[pp][:], 0.0, w=[("U", pp)])
                    memset(U2l[pp][:], 0.0, w=[("U2", pp)], eng="pool" if False else "dve")
                W_ = [2, 4, 8, 16]
                for g in range(4):
                    par = cix[0] % 2
                    cix[0] += 1
                    U, Pa, Pb, dbf = Ul[par], Pal[par], Pbl[par], dbfl[par]
                    kU, kPa, kPb, kd = ("U", par), ("Pa", 0), ("Pb", 0), ("dbf", 0)
                    project(g, U, kU, True)
                    Uv, Pav, Pbv = pview(U), pview(Pa), pview(Pb)
                    tt(Pav[:, :, 0:Lp - 1], Uv[:, :, 0:Lp - 1], Uv[:, :, 1:Lp], ALU.add, r=[kU], w=[kPa])
                    cur, curk, ln, step = Pav, kPa, Lp - 1, 2
                    oth, othk = Pbv, kPb
                    while step < W_[g]:
                        tt(oth[:, :, 0:ln - step], cur[:, :, 0:ln - step], cur[:, :, step:ln], ALU.add, r=[curk], w=[othk])
                        cur, oth, curk, othk = oth, cur, othk, curk
                        ln -= step
                        step *= 2
                    hw = W_[g] // 2
                    icv = (icc[:, g, :].rearrange("p (s t) -> p s t", s=1) if half == 0 else None)
                    for s_ in range(nseq):
                        ic = icc[:, g, :] if half == 0 else icl[:, g, :]
                        tt(oth[:, s_, 0:L], cur[:, s_, 8 - hw:8 - hw + L], ic, ALU.mult, r=[curk, "icc", "icl"], w=[othk])
                    dv = dbf[:, 0:nseq * L].rearrange("p (s t) -> p s t", s=nseq)
                    tt(dv, oth[:, :, 0:L], Uv[:, :, 8:8 + L], ALU.subtract, r=[othk, kU], w=[kd])
                    for (t0, tn, _) in tiles:
                        if t0 >= 2048:
                            continue
                        o = xoff(t0)
                        bank = eit[0] % 2
                        eit[0] += 1
                        mm(PS[bank][:, 0:tn], poolw[:, g, :], dbf[:, o:o + tn], start=True, stop=True,
                           r=["poolw", kd], w=[pk(bank)], inc=True)
                        act(yb[:, g, o:o + tn], PS[bank][:, 0:tn], AF.Copy, r=[pk(bank), "pvec"], w=[("eyb", g)],
                            scale=pvec[:, PV_PSC + g:PV_PSC + g + 1])
                for i in range(4):
                    par = cix[0] % 2
                    cix[0] += 1
                    U, U2, Pa, GB, Ac = Ul[par], U2l[par], Pal[par], GBl[par], Acl[par]
                    kU, kU2, kPa, kGB, kAc = ("U", par), ("U2", par), ("Pa", 0), ("GB", par), ("Ac", 0)
                    project(4 + i, U, kU, True)
                    project(12 + i, U2, kU2, True)
                    project(8 + i, GB, kGB, False)
                    tt(Pa[:, 0:nseq * Lp], U[:, 0:nseq * Lp], U2[:, 0:nseq * Lp], ALU.mult, r=[kU, kU2], w=[kPa])
                    Zv = pview(Pa)
                    Av = Ac[:, 0:nseq * L].rearrange("p (s t) -> p s t", s=nseq)
                    cw = lambda jx: pvec[:, PV_CONV + jx * 4 + i: PV_CONV + jx * 4 + i + 1]
                    ts(Av, Zv[:, :, 8:8 + L], cw(1), ALU.mult, r=[kPa, "pvec"], w=[kAc])
                    for s_ in range(nseq):
                        stt(Av[:, s_, :], Zv[:, s_, 7:7 + L], cw(0), Av[:, s_, :], ALU.mult, ALU.add, r=[kPa, kAc], w=[kAc])
                        stt(Av[:, s_, :], Zv[:, s_, 9:9 + L], cw(2), Av[:, s_, :], ALU.mult, ALU.add, r=[kPa, kAc], w=[kAc])
                    tt(yb[:, 4 + i, 0:1024], GB[:, 0:1024], Ac[:, 0:1024], ALU.mult, r=[kGB, kAc], w=[("eyb", 4 + i)])
                for (t0, tn, _) in tiles:
                    if t0 >= 2048:
                        continue
                    ti = tidx(t0)
                    o = xoff(t0)
                    for m in range(8):
                        bank = 4 + eit[0] % 2
                        eit[0] += 1
                        for k in range(8):
                            mm(PS[bank][:, 0:tn], mixw[:, k, m * 128:(m + 1) * 128], yb[:, k, o:o + tn],
                               start=(k == 0), stop=(k == 7), r=["mixw", ("eyb", k)], w=[pk(bank)], inc=(k == 7))
                        stt(X[:, m, t0:t0 + tn], PS[bank][:, 0:tn], modcol(l, 5, m, cj), X[:, m, t0:t0 + tn],
                            ALU.mult, ALU.add, r=[pk(bank), xk(m, ti)], w=[xk(m, ti)])
            S.barrier()

    def odd_mixer():
        l = 1
        lam_init = 0.8 - 0.6 * math.exp(-0.3 * 1)
        with contextlib.ExitStack() as ph:
            QTl = sb(ph, "QTl", [128, 8, 1024], BF16)
            bvec = sb(ph, "bvec_s", [128, BV_N])
            lamc = sb(ph, "lamc", [128, 4])
            gdn = sb(ph, "gdn", [128, 128])
            small = sb(ph, "small", [128, 8])
            o1s = sb(ph, "o1s", [128, 4, 129])
            dtmp = sb(ph, "dtmp", [128, 128])
            dtmp2 = sb(ph, "dtmp2", [128, 128])
            lt = sb(ph, "lt", [128, 128])
            S.dma("sp", bvec[:], bvec_d, w=["bvec"])
            bl = bvec[:, BV_LAM:BV_LAM + 256].rearrange("p (a d) -> p a d", a=4)
            tt(lt[:, 0:64], bl[:, 0, :], bl[:, 1, :], ALU.mult, r=["bvec"], w=["lt0"])
            tt(lt[:, 64:128], bl[:, 2, :], bl[:, 3, :], ALU.mult, r=["bvec"], w=["lt1"])
            S.op("dve", lambda e: e.reduce_sum(out=lamc[:, 0:2], in_=lt[:].rearrange("p (a d) -> p a d", a=2),
                                               axis=mybir.AxisListType.X), r=["lt0", "lt1"], w=["lamc"])
            act(lamc[:, 0:2], lamc[:, 0:2], AF.Exp, r=["lamc"], w=["lamc"])
            tt(lamc[:, 2:3], lamc[:, 1:2], lamc[:, 0:1], ALU.subtract, r=["lamc"], w=["lamc"])
            ts(lamc[:, 2:3], lamc[:, 2:3], -lam_init, ALU.add, r=["lamc"], w=["lamc"])
            ts(gdn[:], bvec[:, BV_DN:BV_DN + 128], 1.0 - lam_init, ALU.mult, r=["bvec"], w=["gdn"])

            def attn_norm_nat(psb, dst, dkey, rkeys):
                recip(small[:, 0:1], psb[:, 64:65], r=rkeys, w=["small0"])
                ts(dst, psb[:, 0:64], small[:, 0:1], ALU.mult, r=rkeys + ["small0"], w=[dkey])

            def attn_norm_diff(o1, o2ps, dst, dkey, rkeys):
                recip(small[:, 1:2], o1[:, 128:129], r=rkeys, w=["small1"])
                recip(small[:, 2:3], o2ps[:, 128:129], r=rkeys, w=["small2"])
                tt(small[:, 2:3], small[:, 2:3], lamc[:, 2:3], ALU.mult, r=["small2", "lamc"], w=["small2"])
                ts(dtmp[:], o2ps[:, 0:128], small[:, 2:3], ALU.mult, r=rkeys + ["small2"], w=["dtmp"])
                stt(dtmp[:], o1[:, 0:128], small[:, 1:2], dtmp[:], ALU.mult, ALU.add, r=rkeys + ["small1", "dtmp"], w=["dtmp"])
                act(dtmp2[:], dtmp[:], AF.Square, r=["dtmp"], w=["dtmp2", "small3"], accum_out=small[:, 3:4])
                act(small[:, 4:5], small[:, 3:4], AF.Ln, r=["small3"], w=["small4"], bias=epsc[:, 0:1], scale=1.0 / 128)
                act(small[:, 5:6], small[:, 4:5], AF.Exp, r=["small4"], w=["small5"], scale=-0.5)
                stt(dst, dtmp[:], small[:, 5:6], gdn[:], ALU.mult, ALU.mult, r=["dtmp", "small5", "gdn"], w=[dkey])

            oit = [0]
            osl = [0]

            def pipeline(units):
                n_ = len(units)
                if n_:
                    units[0][0]()
                for i_ in range(n_):
                    if i_ + 1 < n_:
                        units[i_ + 1][0]()
                    units[i_][1]()
                    units[i_][2]()

            def out_project(p, otok, nblk, t0, cj):
                ntok = nblk * 128
                oT = sb(p, "oT", [128, 8, ntok], BF16)
                mixw = sb(p, "mixw1", [128, 8, 1024], BF16)
                S.dma("pool", mixw[:], mixw_d[1].rearrange("k p n -> p k n"), w=["mixw1"])
                trn = 0
                for tb in range(nblk):
                    for c in range(8):
                        bank = 2 + trn % 2
                        trn += 1
                        pbf = PS[bank][:].bitcast(BF16)
                        S.op("pe", lambda e: e.transpose(out=pbf[:, 0:128], in_=otok[:, tb, c * 128:(c + 1) * 128], identity=identb[:]),
                             r=[("otok", tb), "identb"], w=[pk(bank)])
                        act(oT[:, c, tb * 128:(tb + 1) * 128], pbf[:, 0:128], AF.Copy, r=[pk(bank)], w=[("oT", c, tb // 4)])
                for tq in range(ntok // 512):
                    ti = tidx(t0 + tq * 512)
                    for m in range(8):
                        bank = oit[0] % 2
                        oit[0] += 1
                        for k in range(8):
                            mm(PS[bank][:, :], mixw[:, k, m * 128:(m + 1) * 128], oT[:, k, tq * 512:(tq + 1) * 512],
                               start=(k == 0), stop=(k == 7), r=["mixw1", ("oT", k, tq)], w=[pk(bank)], inc=(k == 7))
                        stt(X[:, m, t0 + tq * 512:t0 + (tq + 1) * 512], PS[bank][:, :], modcol(l, 5, m, cj),
                            X[:, m, t0 + tq * 512:t0 + (tq + 1) * 512], ALU.mult, ALU.add, r=[pk(bank), xk(m, ti)], w=[xk(m, ti)])

            def norm_bufs(p):
                return (sb(p, "sq", [128, 8, 512], BF16), sb(p, "rt", [128, 512]), sb(p, "rstd", [128, 512]),
                        [sb(p, f"tmp{i}", [128, 512]) for i in range(2)])

            with contextlib.ExitStack() as p1:
                xn = sb(p1, "oxn", [128, 8, 1024], BF16)
                nb = norm_bufs(p1)
                tmp = nb[3]
                ows = [sb(p1, f"ows{i}", [128, 8, 512], BF16) for i in range(3)]
                stgb = [sb(p1, f"stgb{i}", [128, 1024], BF16) for i in range(2)]
                rc_t = sb(p1, "rc_t", [128, 1024])
                rs_t = sb(p1, "rs_t", [128, 1024])
                rtmp = sb(p1, "rtmp", [128, 512])
                S.dma("sp", rc_t[:], ropec_d, w=["rc_t"])
                S.dma("sp", rs_t[:], ropes_d, w=["rs_t"])
                for tq in range(2):
                    modnorm(nb, 1024 + tq * 512, 512, 1, l, 1, 3,
                            out=lambda c, tq=tq: xn[:, c, tq * 512:(tq + 1) * 512], okey=lambda c, tq=tq: ("oxn", c, tq))

                def load_slab(si):
                    slot = osl[0] % 3
                    osl[0] += 1
                    S.dma("pool", ows[slot][:], owin_d[:, :, si * 512:(si + 1) * 512].rearrange("k p n -> p k n"),
                          w=[("ows", slot)])
                    return slot

                gin_keys = []

                def put_gin(row0, src_ap, rkeys):
                    gin_keys.append(("gin", row0))
                    sec_ = row0 // 128
                    S.dma("sp", gin_b[sec_ // 4][(sec_ % 4) * 128:(sec_ % 4 + 1) * 128, :], src_ap, r=rkeys, w=[("gin", row0)])

                def proj_fm(slot, consume):
                    for c in range(4):
                        for tq in range(2):
                            bank = oit[0] % 2
                            oit[0] += 1
                            for k in range(8):
                                mm(PS[bank][:, :], ows[slot][:, k, c * 128:(c + 1) * 128], xn[:, k, tq * 512:(tq + 1) * 512],
                                   start=(k == 0), stop=(k == 7), r=[("ows", slot), ("oxn", k, tq)], w=[pk(bank)], inc=(k == 7))
                            consume(c, tq, bank)

                sl_ = load_slab(0)
                proj_fm(sl_, lambda c, tq, bank: act(QTl[:, c, tq * 512:(tq + 1) * 512], PS[bank][:, :], AF.Copy,
                                                     r=[pk(bank)], w=[("QTl", c, tq)]))
                sl_ = load_slab(1)

                def cons_nk(c, tq, bank):
                    sbuf = stgb[c % 2]
                    act(sbuf[:, tq * 512:(tq + 1) * 512], PS[bank][:, :], AF.Copy, r=[pk(bank)], w=[("stgb", c % 2, tq)])
                    if tq == 1:
                        put_gin(c * 128, sbuf[:], [("stgb", c % 2, 0), ("stgb", c % 2, 1)])
                proj_fm(sl_, cons_nk)
                for (si, sp_, kind) in ((3, 6, "q"), (4, 7, "k")):
                    sA = load_slab(si)
                    sB = load_slab(sp_)
                    for c in range(4):
                        for tq in range(2):
                            b1 = oit[0] % 2
                            b2 = 2 + oit[0] % 2
                            oit[0] += 1
                            for (bk, sl2) in ((b1, sA), (b2, sB)):
                                for k in range(8):
                                    mm(PS[bk][:, :], ows[sl2][:, k, c * 128:(c + 1) * 128], xn[:, k, tq * 512:(tq + 1) * 512],
                                       start=(k == 0), stop=(k == 7), r=[("ows", sl2), ("oxn", k, tq)], w=[pk(bk)], inc=(k == 7))
                            tt(rtmp[:], PS[b1][:, :], rc_t[:, tq * 512:(tq + 1) * 512], ALU.mult, r=[pk(b1), "rc_t"], w=["rtmp"])
                            tt(tmp[0][:], PS[b2][:, :], rs_t[:, tq * 512:(tq + 1) * 512], ALU.mult, r=[pk(b2), "rs_t"], w=[("tmp", 0)])
                            if kind == "q":
                                tt(QTl[:, 4 + c, tq * 512:(tq + 1) * 512], rtmp[:], tmp[0][:], ALU.add,
                                   r=["rtmp", ("tmp", 0)], w=[("QTl", 4 + c, tq)])
                            else:
                                sbuf = stgb[c % 2]
                                tt(sbuf[:, tq * 512:(tq + 1) * 512], rtmp[:], tmp[0][:], ALU.add,
                                   r=["rtmp", ("tmp", 0)], w=[("stgb", c % 2, tq)])
                                if tq == 1:
                                    put_gin((4 + c) * 128, sbuf[:], [("stgb", c % 2, 0), ("stgb", c % 2, 1)])
                for (si, sec0) in ((2, 8), (5, 12)):
                    sl_ = load_slab(si)
                    for tb in range(8):
                        bank = 2 + oit[0] % 2
                        oit[0] += 1
                        for k in range(8):
                            mm(PS[bank][:, :], xn[:, k, tb * 128:(tb + 1) * 128], ows[sl_][:, k, :],
                               start=(k == 0), stop=(k == 7), r=[("ows", sl_), ("oxn", k, tb // 4)], w=[pk(bank)], inc=(k == 7))
                        sbuf = stgb[(tb // 2) % 2]
                        act(sbuf[:, (tb % 2) * 512:(tb % 2 + 1) * 512], PS[bank][:, :], AF.Copy, r=[pk(bank)],
                            w=[("stgb", (tb // 2) % 2, tb % 2)])
                        if tb % 2 == 1:
                            put_gin((sec0 + tb // 2) * 128, sbuf[:], [("stgb", (tb // 2) % 2, 0), ("stgb", (tb // 2) % 2, 1)])
                for t_ in S._deps(gin_keys, []):
                    S.wait("pool", t_)
                S.barrier()
                if True:
                    for ci in range(4):
                        nc.gpsimd.collective_compute("AllGather", ALU.bypass, replica_groups=[[0, 1, 2, 3], [4, 5, 6, 7]],
                                                     ins=[gin_f[ci]], outs=[gout_f[ci]]).then_inc(S.ccsem)


            if CUT < 1.1:
                raise _Cut()
            sit = [0]
            for hc in range(2):
                with contextlib.ExitStack() as p1:
                    t0 = hc * 512
                    xn = sb(p1, "cxn", [128, 8, 512], BF16)
                    nb = norm_bufs(p1)
                    ows = [sb(p1, f"cows{i}", [128, 8, 512], BF16) for i in range(2)]
                    QT = sb(p1, "QT", [128, 8, 512], BF16)
                    KT = sb(p1, "KT", [128, 8, 512], BF16)
                    VN = sb(p1, "VN", [128, 4, 8, 65], BF16)
                    VD = sb(p1, "VD", [128, 4, 4, 129], BF16)
                    stage = [sb(p1, f"stg{i}", [128, 512]) for i in range(2)]
                    ET = [sb(p1, f"ET{i}", [128, 256], BF16) for i in range(4)]
                    otok = sb(p1, "otok", [128, 4, 1024], BF16)
                    memset(VN[:, :, :, 64:65], 1.0, w=["VNones"])
                    memset(VD[:, :, :, 128:129], 1.0, w=["VDones"])
                    modnorm(nb, t0, 512, 0, l, 1, 3, out=lambda c: xn[:, c, :], okey=lambda c: ("cxn", c))

                    def load_slab_c(si):
                        slot = osl[0] % 2
                        osl[0] += 1
                        S.dma("pool", ows[slot][:], owin_d[:, :, si * 512:(si + 1) * 512].rearrange("k p n -> p k n"),
                              w=[("cows", slot)])
                        return slot

                    if CUT < 1.12:
                        raise _Cut()
                    for (si, isq, ch0) in ((0, True, 0), (1, False, 0), (3, True, 4), (4, False, 4)):
                        slot = load_slab_c(si)
                        for c in range(4):
                            bank = oit[0] % 2
                            oit[0] += 1
                            for k in range(8):
                                mm(PS[bank][:, :], ows[slot][:, k, c * 128:(c + 1) * 128], xn[:, k, :],
                                   start=(k == 0), stop=(k == 7), r=[("cows", slot), ("cxn", k)], w=[pk(bank)], inc=(k == 7))
                            dstT = QT if isq else KT
                            act(dstT[:, ch0 + c, :], PS[bank][:, :], AF.Copy, r=[pk(bank)], w=[("QT" if isq else "KT", ch0 + c)])
                    if CUT < 1.14:
                        raise _Cut()
                    for (si, od, vdst) in ((1, onk_d, None), (2, onv_d, "n"), (4, odk_d, None), (5, odv_d, "d")):
                        if CUT < 1.16 and si == 2:
                            raise _Cut()
                        slot = load_slab_c(si)
                        for tb in range(4):
                            bank = 2 + oit[0] % 2
                            oit[0] += 1
                            for k in range(8):
                                mm(PS[bank][:, :], xn[:, k, tb * 128:(tb + 1) * 128], ows[slot][:, k, :],
                                   start=(k == 0), stop=(k == 7), r=[("cows", slot), ("cxn", k)], w=[pk(bank)], inc=(k == 7))
                            sx = sit[0] % 2
                            sit[0] += 1
                            act(stage[sx][:], PS[bank][:, :], AF.Copy, r=[pk(bank)], w=[("stage", sx)])
                            if True:
                                tk = S.dma("sp", od[t0 + tb * 128:t0 + (tb + 1) * 128, :], stage[sx][:], r=[("stage", sx)])
                                S.out_toks.append(tk)
                            if vdst == "n":
                                act(VN[:, tb, :, 0:64], PS[bank][:, :].rearrange("p (h d) -> p h d", h=8), AF.Copy,
                                    r=[pk(bank)], w=[("VN", tb)])
                            elif vdst == "d":
                                act(VD[:, tb, :, 0:128], PS[bank][:, :].rearrange("p (h d) -> p h d", h=4), AF.Copy,
                                    r=[pk(bank)], w=[("VD", tb)])
                    units = []
                    ui = 0
                    for s_ in range(2):
                        q0 = s_ * 256
                        for h in range(8):
                            def A(s_=s_, h=h, q0=q0, ui=ui):
                                pb0 = 64 * (h % 2)
                                for kb in range(2):
                                    bank = (ui % 2) * 2 + kb
                                    mm(PS[bank][:, 0:256], KT[pb0:pb0 + 64, h // 2, q0 + kb * 128:q0 + (kb + 1) * 128],
                                       QT[pb0:pb0 + 64, h // 2, q0:q0 + 256], start=True, stop=True,
                                       r=[("KT", h // 2), ("QT", h // 2)], w=[pk(bank)], inc=True)

                            def B(ui=ui):
                                for kb in range(2):
                                    bank = (ui % 2) * 2 + kb
                                    ei = (ui % 2) * 2 + kb
                                    act(ET[ei][:], PS[bank][:, 0:256], AF.Exp, r=[pk(bank)], w=[("ET", ei)], scale=0.125)

                            def C(s_=s_, h=h, ui=ui):
                                for qb in range(2):
                                    pbk = 4 + (h * 2 + qb) % 4
                                    for kb in range(2):
                                        ei = (ui % 2) * 2 + kb
                                        mm(PS[pbk][:, 0:65], ET[ei][:, qb * 128:(qb + 1) * 128], VN[:, s_ * 2 + kb, h, :],
                                           start=(kb == 0), stop=(kb == 1), r=[("ET", ei), ("VN", s_ * 2 + kb), "VNones"],
                                           w=[pk(pbk)], inc=(kb == 1))
                                    attn_norm_nat(PS[pbk], otok[:, s_ * 2 + qb, h * 64:(h + 1) * 64], ("otok", s_ * 2 + qb), [pk(pbk)])
                            units.append((A, B, C))
                            ui += 1
                        for h in range(4):
                            for j in range(2):
                                def A(s_=s_, h=h, j=j, q0=q0, ui=ui):
                                    pb0 = 64 * j
                                    for kb in range(2):
                                        bank = (ui % 2) * 2 + kb
                                        mm(PS[bank][:, 0:256], KT[pb0:pb0 + 64, 4 + h, q0 + kb * 128:q0 + (kb + 1) * 128],
                                           QT[pb0:pb0 + 64, 4 + h, q0:q0 + 256], start=True, stop=True,
                                           r=[("KT", 4 + h), ("QT", 4 + h)], w=[pk(bank)], inc=True)

                                def B(ui=ui):
                                    for kb in range(2):
                                        bank = (ui % 2) * 2 + kb
                                        ei = (ui % 2) * 2 + kb
                                        act(ET[ei][:], PS[bank][:, 0:256], AF.Exp, r=[pk(bank)], w=[("ET", ei)], scale=0.125)

                                def C(s_=s_, h=h, j=j, ui=ui):
                                    for qb in range(2):
                                        pbk = 4 + 2 * qb + j
                                        for kb in range(2):
                                            ei = (ui % 2) * 2 + kb
                                            mm(PS[pbk][:, 0:129], ET[ei][:, qb * 128:(qb + 1) * 128], VD[:, s_ * 2 + kb, h, :],
                                               start=(kb == 0), stop=(kb == 1), r=[("ET", ei), ("VD", s_ * 2 + kb), "VDones"],
                                               w=[pk(pbk)], inc=(kb == 1))
                                    if j == 1:
                                        for qb in range(2):
                                            pj0 = 4 + 2 * qb
                                            act(o1s[:, qb, :], PS[pj0][:, 0:129], AF.Copy, r=[pk(pj0)], w=[("o1s", qb)])
                                            attn_norm_diff(o1s[:, qb, :], PS[pj0 + 1][:, 0:129],
                                                           otok[:, s_ * 2 + qb, 512 + h * 128:512 + (h + 1) * 128], ("otok", s_ * 2 + qb),
                                                           [("o1s", qb), pk(pj0 + 1)])
                                units.append((A, B, C))
                                ui += 1
                    pipeline(units)
                    if CUT < 1.8:
                        raise _Cut()
                    out_project(p1, otok, 4, t0, 0)
                    S.barrier()

            if CUT < 3:
                raise _Cut()
            for q_ in ("sp", "pool"):
                S.wait(q_, (S.ccsem, 4, None))
            with contextlib.ExitStack() as pl:
                otok = sb(pl, "otokl", [128, 8, 1024], BF16)
                ETl = [sb(pl, f"ETl{i}", [128, 512], BF16) for i in range(3)]
                PTl = [sb(pl, f"PTl{i}", [128, 512], BF16) for i in range(3)]
                gviews = [g_.rearrange("(r s p) n -> r s p n", r=4, s=4) for g_ in gout_b]
                with contextlib.ExitStack() as p2:
                    KNw = sb(p2, "KNw", [128, 4, 1536], BF16)
                    VNw = sb(p2, "VNw", [128, 12, 8, 65], BF16)
                    KNc = sb(p2, "KNc", [128, 4, 512], BF16)
                    VNc = sb(p2, "VNc", [128, 4, 8, 65], BF16)
                    memset(VNw[:, :, :, 64:65], 1.0, w=["VNw1"])
                    memset(VNc[:, :, :, 64:65], 1.0, w=["VNc1"])
                    S.dma("sp", KNw[:, :, 256:1280], gin_b[0][0:512, :].rearrange("(c p) t -> p c t", p=128), w=["KNw_own"])
                    for sec in range(4):
                        for hf in range(2):
                            S.dma("sp", VNw[:, 2 + sec * 2 + hf, :, 0:64],
                                  gin_b[2][sec * 128:(sec + 1) * 128, hf * 512:(hf + 1) * 512].rearrange("p (h d) -> p h d", h=8),
                                  w=[("VNw_own", sec, hf)])
                    S.dma("pool", KNc[:], cnkT_d.rearrange("(c p) k -> p c k", p=128), w=["KNc"])
                    for kb in range(4):
                        S.dma("pool", VNc[:, kb, :, 0:64], cnv_d[kb * 128:(kb + 1) * 128, :].rearrange("p (h d) -> p h d", h=8),
                              w=[("VNc", kb)])
                    with contextlib.ExitStack() as p2a:
                        Gk = [sb(p2a, f"Gk{i}", [128, 4, 4, 256], BF16) for i in range(2)]
                        Gv = [sb(p2a, f"Gv{i}", [128, 4, 1024], BF16) for i in range(2)]
                        for side in range(2):
                            cols = slice(768, 1024) if side == 0 else slice(0, 256)
                            for rr in range(4):
                                S.dma("sp", Gk[side][:, rr, :, :],
                                      gout_b[0][rr * 512:rr * 512 + 512, cols].rearrange("(c p) t -> p c t", p=128), w=[("Gk", side, rr)])
                            sec = 3 if side == 0 else 0
                            S.dma("sp", Gv[side][:], gviews[2][:, sec].rearrange("r p n -> p r n"), w=[("Gv", side)])
                            selo = PV_SELP if side == 0 else PV_SELN
                            kdst = KNw[:, :, 0:256] if side == 0 else KNw[:, :, 1280:1536]
                            kkey = ("KNw_h", side)
                            ts(kdst, Gk[side][:, 0, :, :], pvec[:, selo:selo + 1], ALU.mult, r=[("Gk", side, 0), "pvec"], w=[kkey])
                            for rr in range(1, 4):
                                stt(kdst, Gk[side][:, rr, :, :], pvec[:, selo + rr:selo + rr + 1], kdst, ALU.mult, ALU.add,
                                    r=[("Gk", side, rr), "pvec", kkey], w=[kkey])
                            for blk in range(2):
                                vdst = VNw[:, (0 if side == 0 else 10) + blk, :, 0:64]
                                vkey = ("VNw_h", side, blk)
                                src = lambda rr: Gv[side][:, rr, blk * 512:(blk + 1) * 512].rearrange("p (h d) -> p h d", h=8)
                                ts(vdst, src(0), pvec[:, selo:selo + 1], ALU.mult, r=[("Gv", side), "pvec"], w=[vkey])
                                for rr in range(1, 4):
                                    stt(vdst, src(rr), pvec[:, selo + rr:selo + rr + 1], vdst, ALU.mult, ALU.add,
                                        r=[("Gv", side), "pvec", vkey], w=[vkey])
                        S.barrier()
                    TB2 = sb(p2, "TB2", [128, 8, 18, 64], BF16)
                    tsrc = [sb(p2, f"tsrc{i}", [128, 15, 64]) for i in range(2)]
                    amat = sb(p2, "amat_s", [128, 1536], BF16)
                    bmat = sb(p2, "bmat_s", [128, 1024], BF16)
                    memset(amat[:], 0.0, w=["amat0"])
                    memset(bmat[:], 0.0, w=["bmat0"])
                    S.dma("pool", amat[0:24, :], amat_d, r=["amat0"], w=["amat"])
                    S.dma("pool", bmat[0:24, :], bmat_d, r=["bmat0"], w=["bmat"])
                    memset(TB2[:], 0.0, w=["TB2z"])
                    cmb = bvec[:, BV_CM:BV_CM + 64].unsqueeze(1).broadcast_to([128, 15, 64])
                    cmb_lo = bvec[0:64, BV_CM:BV_CM + 64].unsqueeze(1).broadcast_to([64, 15, 64])
                    cmb_hi = bvec[64:128, BV_CM:BV_CM + 64].unsqueeze(1).broadcast_to([64, 15, 64])
                    for h in range(8):
                        tsb = tsrc[h % 2]
                        srcv = tbsrc_d[:, h * 960:(h + 1) * 960].rearrange("p (a b) -> p a b", a=15)
                        S.dma("sp", tsb[0:64], srcv, w=[("tsrc", h % 2, 0)])
                        S.dma("sp", tsb[64:128], srcv, w=[("tsrc", h % 2, 1)])
                        act(tsb[:], tsb[:], AF.Exp, r=[("tsrc", h % 2, 0), ("tsrc", h % 2, 1)], w=[("tsrc", h % 2, 0), ("tsrc", h % 2, 1)])
                        tt(TB2[0:64, h, 1:16, :], tsb[0:64], cmb_lo, ALU.mult, r=[("tsrc", h % 2, 0), "bvec", "TB2z"], w=[("TB2", h, 0)])
                        tt(TB2[64:128, h, 2:17, :], tsb[64:128], cmb_hi, ALU.mult, r=[("tsrc", h % 2, 1), "bvec", "TB2z"], w=[("TB2", h, 1)])
                    nat_r_k = ["KNw_own", ("KNw_h", 0), ("KNw_h", 1)]
                    QZn = sb(p2, "QZn", [128, 8, 1024], BF16)
                    memset(QZn[:], 0.0, w=["QZn0"])
                    for h in range(8):
                        pb0 = 64 * (h % 2)
                        if h % 2 == 0:
                            ts(QZn[pb0:pb0 + 64, h, :], QTl[pb0:pb0 + 64, h // 2, :], 1.0, ALU.mult,
                               r=[("QTl", h // 2, 0), ("QTl", h // 2, 1), "QZn0"], w=[("QZn", h)])
                        else:
                            act(QZn[pb0:pb0 + 64, h, :], QTl[pb0:pb0 + 64, h // 2, :], AF.Copy,
                                r=[("QTl", h // 2, 0), ("QTl", h // 2, 1), "QZn0"], w=[("QZn", h)])
                    units = []
                    ui = 0
                    for h in range(8):
                        for qt in range(2):
                            started = [False] * 4
                            for kb in range(12):
                                qbs = [qb for qb in range(4 * qt, 4 * qt + 4) if -2 <= kb - qb <= 6]
                                if not qbs:
                                    continue
                                firsts = [not started[qb % 4] for qb in qbs]
                                for qb in qbs:
                                    started[qb % 4] = True

                                def A(h=h, kb=kb, qbs=qbs, ui=ui):
                                    pb0 = 64 * (h % 2)
                                    qlo, qhi = qbs[0], qbs[-1]
                                    n = 2 * (qhi - qlo + 1) * 64
                                    tq0 = qlo * 128
                                    bank = ui % 2
                                    mm(PS[bank][:, 0:n], KNw[:, h // 2, kb * 128:(kb + 1) * 128],
                                       QZn[:, h, tq0:tq0 + n], start=True, stop=False,
                                       r=nat_r_k + [("QZn", h), "QZn0"], w=[pk(bank)], inc=False)
                                    mm(PS[bank][:, 0:n], amat[:, kb * 128:(kb + 1) * 128], bmat[:, tq0:tq0 + n],
                                       start=False, stop=True, r=["amat", "bmat", "amat0", "bmat0"], w=[pk(bank)], inc=True)

                                def B(h=h, kb=kb, qbs=qbs, ui=ui):
                                    qlo, qhi = qbs[0], qbs[-1]
                                    nrows = 2 * (qhi - qlo + 1)
                                    n = nrows * 64
                                    bank = ui % 2
                                    e_, p_ = ETl[ui % 3], PTl[ui % 3]
                                    act(e_[:, 0:n], PS[bank][:, 0:n], AF.Exp, r=[pk(bank)], w=[("ETl", ui % 3)], scale=0.125)
                                    s0 = 12 - 2 * kb + 2 * qlo
                                    tt(p_[:, 0:n], e_[:, 0:n], TB2[:, h, s0:s0 + nrows, :].rearrange("p a b -> p (a b)"), ALU.mult,
                                       r=[("ETl", ui % 3), ("TB2", h, 0), ("TB2", h, 1), "TB2z"], w=[("PTl", ui % 3)])

                                def C(h=h, kb=kb, qbs=qbs, firsts=firsts, ui=ui):
                                    qlo = qbs[0]
                                    p_ = PTl[ui % 3]
                                    vkeys = ["VNw1"]
                                    if kb < 2:
                                        vkeys.append(("VNw_h", 0, kb))
                                    elif kb >= 10:
                                        vkeys.append(("VNw_h", 1, kb - 10))
                                    else:
                                        vkeys.append(("VNw_own", (kb - 2) // 2, (kb - 2) % 2))
                                    for qb, fst in zip(qbs, firsts):
                                        qi = qb % 4
                                        mm(PS[4 + qi][:, 0:65], p_[:, (qb - qlo) * 128:(qb - qlo + 1) * 128], VNw[:, kb, h, :],
                                           start=fst, stop=False, r=[("PTl", ui % 3)] + vkeys, w=[pk(4 + qi)], inc=False)
                                units.append((A, B, C))
                                ui += 1
                            for kb in range(4):
                                def A(h=h, kb=kb, qt=qt, ui=ui):
                                    pb0 = 64 * (h % 2)
                                    bank = ui % 2
                                    mm(PS[bank][:, :], KNc[:, h // 2, kb * 128:(kb + 1) * 128],
                                       QZn[:, h, qt * 512:(qt + 1) * 512], start=True, stop=True,
                                       r=["KNc", ("QZn", h), "QZn0"], w=[pk(bank)], inc=True)

                                def B(ui=ui):
                                    bank = ui % 2
                                    act(ETl[ui % 3][:], PS[bank][:, :], AF.Exp, r=[pk(bank)], w=[("ETl", ui % 3)], scale=0.125)

                                def C(h=h, kb=kb, qt=qt, ui=ui):
                                    e_ = ETl[ui % 3]
                                    for qi in range(4):
                                        mm(PS[4 + qi][:, 0:65], e_[:, qi * 128:(qi + 1) * 128], VNc[:, kb, h, :],
                                           start=False, stop=(kb == 3), r=[("ETl", ui % 3), ("VNc", kb), "VNc1"], w=[pk(4 + qi)],
                                           inc=(kb == 3))
                                    if kb == 3:
                                        for qi in range(4):
                                            qb = 4 * qt + qi
                                            attn_norm_nat(PS[4 + qi], otok[:, qb, h * 64:(h + 1) * 64], ("otok", qb), [pk(4 + qi)])
                                units.append((A, B, C))
                                ui += 1
                    pipeline(units)
                    S.barrier()
                if CUT < 4:
                    raise _Cut()
                with contextlib.ExitStack() as p3:
                    KDh = [sb(p3, f"KDh{i}", [128, 4096], BF16) for i in range(2)]
                    VDh = [sb(p3, f"VDh{i}", [128, 32, 129], BF16) for i in range(2)]
                    KDc = sb(p3, "KDc", [128, 4, 512], BF16)
                    VDc = sb(p3, "VDc", [128, 4, 4, 129], BF16)
                    for i in range(2):
                        memset(VDh[i][:, :, 128:129], 1.0, w=[("VDh1", i)])
                    memset(VDc[:, :, :, 128:129], 1.0, w=["VDc1"])
                    S.dma("pool", KDc[:], cdkT_d.rearrange("(c p) k -> p c k", p=128), w=["KDc"])
                    for kb in range(4):
                        S.dma("pool", VDc[:, kb, :, 0:128], cdv_d[kb * 128:(kb + 1) * 128, :].rearrange("p (h d) -> p h d", h=4),
                              w=[("VDc", kb)])
                    QZd = sb(p3, "QZd", [128, 4, 2, 1024], BF16)
                    memset(QZd[:], 0.0, w=["QZd0"])
                    for h in range(4):
                        ts(QZd[0:64, h, 0, :], QTl[0:64, 4 + h, :], 1.0, ALU.mult,
                           r=[("QTl", 4 + h, 0), ("QTl", 4 + h, 1), "QZd0"], w=[("QZd", h, 0)])
                        act(QZd[64:128, h, 1, :], QTl[64:128, 4 + h, :], AF.Copy,
                            r=[("QTl", 4 + h, 0), ("QTl", 4 + h, 1), "QZd0"], w=[("QZd", h, 1)])

                    def load_head(h):
                        hb_ = h % 2
                        S.dma("sp", KDh[hb_][:].rearrange("p (r t) -> p r t", r=4), gviews[1][:, h].rearrange("r p t -> p r t"),
                              w=[("KDh", hb_)])
                        for rr in range(4):
                            for s4 in range(4):
                                r0_ = rr * 512 + s4 * 128
                                srcv = gout_b[3][r0_:r0_ + 128, :].rearrange("p (hf x) -> p hf x", hf=2)
                                S.dma("sp", VDh[hb_][:, rr * 8 + 2 * s4:rr * 8 + 2 * s4 + 2, 0:128],
                                      srcv[:, :, h * 128:(h + 1) * 128], w=[("VDh", hb_, rr * 4 + s4)])

                    load_head(0)
                    load_head(1)
                    units = []
                    ui = 0
                    for h in range(4):
                        hb_ = h % 2
                        for qt in range(2):
                            for j in range(2):
                                for kb in range(36):
                                    def A(h=h, hb_=hb_, qt=qt, j=j, kb=kb, ui=ui):
                                        bank = ui % 2
                                        if kb < 32:
                                            lhs = KDh[hb_][:, kb * 128:(kb + 1) * 128]
                                            lk_ = [("KDh", hb_)]
                                        else:
                                            lhs = KDc[:, h, (kb - 32) * 128:(kb - 31) * 128]
                                            lk_ = ["KDc"]
                                        mm(PS[bank][:, :], lhs, QZd[:, h, j, qt * 512:(qt + 1) * 512], start=True, stop=True,
                                           r=lk_ + [("QZd", h, j), "QZd0"], w=[pk(bank)], inc=True)

                                    def B(ui=ui):
                                        bank = ui % 2
                                        act(ETl[ui % 3][:], PS[bank][:, :], AF.Exp, r=[pk(bank)], w=[("ETl", ui % 3)], scale=0.125)

                                    def C(h=h, hb_=hb_, qt=qt, j=j, kb=kb, ui=ui):
                                        e_ = ETl[ui % 3]
                                        if kb < 32:
                                            vv = VDh[hb_][:, kb, :]
                                            vk_ = [("VDh", hb_, kb // 2), ("VDh1", hb_)]
                                        else:
                                            vv = VDc[:, kb - 32, h, :]
                                            vk_ = [("VDc", kb - 32), "VDc1"]
                                        for qi in range(4):
                                            mm(PS[4 + qi][:, 0:129], e_[:, qi * 128:(qi + 1) * 128], vv,
                                               start=(kb == 0), stop=(kb == 35), r=[("ETl", ui % 3)] + vk_, w=[pk(4 + qi)], inc=(kb == 35))
                                        if kb == 35:
                                            if j == 0:
                                                for qi in range(4):
                                                    act(o1s[:, qi, :], PS[4 + qi][:, 0:129], AF.Copy, r=[pk(4 + qi)], w=[("o1s", qi)])
                                            else:
                                                for qi in range(4):
                                                    qb = 4 * qt + qi
                                                    attn_norm_diff(o1s[:, qi, :], PS[4 + qi][:, 0:129],
                                                                   otok[:, qb, 512 + h * 128:512 + (h + 1) * 128], ("otok", qb),
                                                                   [("o1s", qi), pk(4 + qi)])
                                                if qt == 1 and h + 2 < 4:
                                                    load_head(h + 2)
                                    units.append((A, B, C))
                                    ui += 1
                    pipeline(units)
                    S.barrier()
                if CUT < 5:
                    raise _Cut()
                with contextlib.ExitStack() as p4:
                    out_project(p4, otok, 8, 1024, 1)
                    S.barrier()
            S.barrier()

    def final_norm():
        with contextlib.ExitStack() as ph:
            bv = sb(ph, "fnb", [128, 1024])
            S.dma("sp", bv[:], bvec_d[:, BV_FN:BV_FN + 1024], w=["fnb"])
            tk = [sb(ph, f"ftk{i}", [128, 1024]) for i in range(2)]
            ob = [sb(ph, f"fob{i}", [128, 1024]) for i in range(2)]
            fsq = sb(ph, "fsq", [128, 1024])
            sm = sb(ph, "fsm", [128, 2, 4])
            for tb in range(16):
                par = tb % 2
                ti = tidx(tb * 128)
                for c in range(8):
                    bank = par * 2 + c // 4
                    S.op("pe", lambda e, bank=bank, c=c, tb=tb: e.transpose(
                        out=PS[bank][:, (c % 4) * 128:(c % 4 + 1) * 128], in_=X[:, c, tb * 128:(tb + 1) * 128], identity=ident[:]),
                        r=[xk(c, ti), "ident"], w=[pk(bank)])
                for hf in range(2):
                    act(tk[par][:, hf * 512:(hf + 1) * 512], PS[par * 2 + hf][:, :], AF.Copy, r=[pk(par * 2 + hf)], w=[("ftk", par, hf)])
                act(fsq[:], tk[par][:], AF.Square, r=[("ftk", par, 0), ("ftk", par, 1)], w=["fsq", ("fsm", par, 0)],
                    accum_out=sm[:, par, 0:1])
                act(sm[:, par, 1:2], sm[:, par, 0:1], AF.Sqrt, r=[("fsm", par, 0)], w=[("fsm", par, 1)], bias=epsc[:, 0:1], scale=1.0 / D)
                recip(sm[:, par, 2:3], sm[:, par, 1:2], r=[("fsm", par, 1)], w=[("fsm", par, 2)])
                stt(ob[par][:], tk[par][:], sm[:, par, 2:3], bv[:], ALU.mult, ALU.mult,
                    r=[("ftk", par, 0), ("ftk", par, 1), ("fsm", par, 2), "fnb"], w=[("fob", par)])
                tk_ = S.dma("sp", y_d[tb * 128:(tb + 1) * 128, :], ob[par][:], r=[("fob", par)])
                S.out_toks.append(tk_)
            S.barrier()

    ffn(0, 0, TILES_ALL + [TILE_HALO])
    even_mixer()
    ffn(0, 1, TILES_ALL)
    if stage == "full":
        ffn(1, 0, TILES_ALL)
        try:
            odd_mixer()
            ffn(1, 1, TILES_ALL)
        except _Cut:
            S.barrier()
            cut_hit.append(1)
    final_norm()
    for t in S.out_toks:
        S.wait("sp", t)
    S.barrier()
    if not cut_hit:
        es.close()
    return nc


def _cnt_inv(w, t, n):
    hw = w // 2
    lo = np.clip(t - hw, 0, n - 1)
    hi = np.clip(t + hw - 1, 0, n - 1)
    return (1.0 / (hi - lo + 1)).astype(np.float32)


def _partner():
    p = np.arange(128)
    within = p % 64
    part = (within % 32) // 16
    return np.where(part == 0, p + 16, p - 16), part, within


def prepare_inputs(inp):
    f = lambda a: np.ascontiguousarray(np.asarray(a, dtype=np.float32))
    g = {k: np.asarray(v) for k, v in inp.items()}
    p = np.arange(128)
    shared = {}
    modw_all = g["mod_w"].reshape(2, 8, 128, 8, 1152).transpose(3, 0, 2, 1, 4)
    w1in = np.empty((2, 2, NJ, 128, 2048), np.float32)
    w1out = np.empty((2, 2, NJ, 128, 1024), np.float32)
    for l in range(2):
        for wi, nm in enumerate(("ffn1", "ffn2")):
            W = g[nm + "_w_in"][l]
            a = W[:, :DFF].reshape(8, 128, NJ, 128).transpose(2, 1, 0, 3)
            b = W[:, DFF:].reshape(8, 128, NJ, 128).transpose(2, 1, 0, 3)
            w1in[l, wi] = np.concatenate([a, b], axis=3).reshape(NJ, 128, 2048)
            w1out[l, wi] = g[nm + "_w_out"][l].reshape(NJ, 128, 1024)
    shared["w1in"] = w1in
    shared["w1out"] = w1out
    shared["ewin"] = f(g["even_w_in"][0].reshape(8, 128, 2048))
    shared["poolw"] = f(g["pool_w"][0])
    shared["mixw"] = f(g["mix_w_out"].reshape(2, 8, 128, 1024))
    partner, part, within = _partner()
    W = g["odd_w_in"][0]
    permidx = (np.arange(4)[:, None] * 128 + partner[None, :]).reshape(-1)
    Wext = np.concatenate([W, W[:, 1536 + permidx], W[:, 2048 + permidx]], axis=1)
    shared["owin"] = f(Wext.reshape(8, 128, 4096))
    shared["ident"] = np.eye(128, dtype=np.float32)
    W4 = [2, 4, 8, 16]
    shared["icc"] = f(np.broadcast_to(np.concatenate([_cnt_inv(w, np.arange(256), 256) for w in W4])[None, :], (128, 1024)))
    rpb = g["nat_rpb"][0]
    kc = np.arange(64)[:, None]
    qc = np.arange(64)[None, :]
    dc = np.clip(kc - qc + 15, 0, 30)
    tb = rpb[:, ::-1, :][:, :, dc]
    shared["tbsrc"] = f(tb.transpose(2, 0, 1, 3).reshape(64, 8 * 15 * 64))
    shared["amat"] = f((np.arange(1536)[None, :] // 64 == np.arange(24)[:, None]).astype(np.float32))
    c0 = np.clip(np.arange(64) - 8, 0, 48)
    colmask = ((kc >= c0[None, :]) & (kc <= c0[None, :] + 15)).astype(np.float32)
    per_core = []
    quarter = 16
    inv = 1.0 / (10000.0 ** (np.arange(quarter) / quarter))
    for c in range(8):
        b, r = c // 4, c % 4
        d = dict(shared)
        xin = np.zeros((NTOK, D), np.float32)
        xin[0:1024] = g["x_prompt"][4 * c:4 * c + 4].reshape(1024, D)
        xin[1024:2048] = g["x_sample"][b, 1024 * r:1024 * (r + 1)]
        if r > 0:
            xin[2048:2056] = g["x_sample"][b, 1024 * r - 8:1024 * r]
        if r < 3:
            xin[2056:2064] = g["x_sample"][b, 1024 * (r + 1):1024 * (r + 1) + 8]
        d["xin"] = xin
        pv = np.zeros((128, PV_N), np.float32)
        d["modw"] = f(np.stack([modw_all[r + 4 * sl, l] for sl in range(2) for l in range(2)]).reshape(4, 128, 8 * 1152))
        conds = [g["c_ctx"], g["c"][0], g["c"][1]]
        for j in range(3):
            pv[:, PV_COND + np.arange(8) * 3 + j] = conds[j].reshape(8, 128).T
        pv[:, PV_SELB + b] = 1.0
        for sl in range(2):
            for l in range(2):
                q4 = sl * 2 + l
                slab = r + 4 * sl
                pv[:, PV_MODB + q4 * 9:PV_MODB + (q4 + 1) * 9] = g["mod_b"][l].reshape(72, 128)[slab * 9:(slab + 1) * 9].T
        for l in range(2):
            for n, nm in enumerate(("norm_ffn1", "norm_mix", "norm_ffn2")):
                o = PV_GAIN + (l * 3 + n) * 8
                pv[:, o:o + 8] = g[nm][l].reshape(8, 128).T
        pv[:, PV_PSC:PV_PSC + 4] = g["pool_scale"][0].reshape(4, 128).T
        for j in range(3):
            pv[:, PV_CONV + j * 4:PV_CONV + j * 4 + 4] = g["conv_w"][0, j].reshape(4, 128).T
        pv[:, PV_HMASK:PV_HMASK + 8] = 1.0 if r > 0 else 0.0
        pv[:, PV_HMASK + 8:PV_HMASK + 16] = 1.0 if r < 3 else 0.0
        if r > 0:
            pv[:, PV_SELP + r - 1] = 1.0
        if r < 3:
            pv[:, PV_SELN + r + 1] = 1.0
        d["pvec"] = pv
        bv = np.zeros((128, BV_N), np.float32)
        bv[:, BV_FN:BV_FN + 1024] = g["final_norm"][None, :]
        bv[:, BV_DN:BV_DN + 128] = g["diff_norm"][0][None, :]
        bv[:, BV_LAM:BV_LAM + 256] = g["diff_lambda"][0].reshape(-1)[None, :]
        bv[:, BV_CM:BV_CM + 64] = colmask[p % 64, :]
        d["bvec"] = bv
        tabs = 1024 * r + np.arange(1024)
        d["icl"] = f(np.broadcast_to(np.concatenate([_cnt_inv(w, tabs, 4096) for w in W4])[None, :], (128, 4096)))
        row = (tabs // 64).astype(np.float64)
        col = (tabs % 64).astype(np.float64)
        a_ = within // 32
        i_ = within % 16
        pos = np.where(a_[:, None] == 0, row[None, :], col[None, :])
        ang = pos * inv[i_][:, None]
        d["ropec"] = np.cos(ang).astype(np.float32)
        d["ropes"] = (np.sin(ang).astype(np.float32) * np.where(part == 0, -1.0, 1.0)[:, None]).astype(np.float32)
        R0 = 16 * r
        lk = np.arange(24)[:, None]
        lq = (np.arange(1024) // 64)[None, :]
        kr = R0 - 4 + lk
        qr = R0 + lq
        r0 = np.clip(qr - 4, 0, 56)
        valid = (kr >= 0) & (kr <= 63) & (kr >= r0) & (kr <= r0 + 7)
        d["bmat"] = np.where(valid, 0.0, NEG).astype(np.float32)
        d["cnkT"] = f(g["cache_nat_k"][b, 0].reshape(512, 512).T)
        d["cnv"] = f(g["cache_nat_v"][b, 0].reshape(512, 512))
        d["cdkT"] = f(g["cache_diff_k"][b, 0].reshape(512, 512).T)
        d["cdv"] = f(g["cache_diff_v"][b, 0].reshape(512, 512))
        per_core.append(d)
    return per_core


def assemble(results):
    y_prompt = np.empty((32, 256, D), np.float32)
    y_sample = np.empty((2, 4096, D), np.float32)
    nnk = np.empty((32, 1, 256, 8, 64), np.float32)
    nnv = np.empty((32, 1, 256, 8, 64), np.float32)
    ndk = np.empty((32, 1, 256, 4, 2, 64), np.float32)
    ndv = np.empty((32, 1, 256, 4, 128), np.float32)
    for c in range(8):
        b, r = c // 4, c % 4
        res = results[c]
        y = np.asarray(res["y"])
        y_prompt[4 * c:4 * c + 4] = y[0:1024].reshape(4, 256, D)
        y_sample[b, 1024 * r:1024 * (r + 1)] = y[1024:2048]
        nnk[4 * c:4 * c + 4, 0] = np.asarray(res["onk"]).reshape(4, 256, 8, 64)
        nnv[4 * c:4 * c + 4, 0] = np.asarray(res["onv"]).reshape(4, 256, 8, 64)
        ndk[4 * c:4 * c + 4, 0] = np.asarray(res["odk"]).reshape(4, 256, 4, 2, 64)
        ndv[4 * c:4 * c + 4, 0] = np.asarray(res["odv"]).reshape(4, 256, 4, 128)
    return (y_prompt, y_sample, nnk, nnv, ndk, ndv)


def kernel(stage="full", **inputs):
    per_core = prepare_inputs(inputs)
    nc = build_program(stage)
    res = run_bass_kernel_spmd(nc, per_core, core_ids=list(range(8)))
    return assemble(res.results)
```
